# Optimizing a Trainium2 kernel written in Bass

```python
import jax, jax.numpy as jnp
from jax import lax
import numpy as np

D_MODEL = 1024
BATCH = 8
SEQ = 8192
DEPTH = 1

N_META = 16
BLOCK = 128
WINDOW = 128
HEAD_DIM = 64
ATT_Q_HEADS = D_MODEL // HEAD_DIM
ATT_KV_HEADS = max(ATT_Q_HEADS // 8, 1)
ATT_GROUP = ATT_Q_HEADS // ATT_KV_HEADS
ATT_WIDTH = ATT_Q_HEADS * HEAD_DIM
ATT_KV_WIDTH = ATT_KV_HEADS * HEAD_DIM
ROPE_THETA = 10000.0
RWKV_HEAD = 64
RWKV_HEADS = D_MODEL // RWKV_HEAD
RWKV_WIDTH = RWKV_HEADS * RWKV_HEAD
DECAY_LORA = 64
ICLR_LORA = 64
RWKV_SHIFT_WIDTH = 3 * RWKV_WIDTH + DECAY_LORA + ICLR_LORA
RMS_EPS = 1e-6
GN_EPS = 64e-5
NEG_INF = -1e30
SPLIT_SIZES = (ATT_WIDTH, ATT_KV_WIDTH, ATT_KV_WIDTH, ATT_WIDTH, RWKV_WIDTH, RWKV_WIDTH, RWKV_WIDTH, DECAY_LORA, ICLR_LORA, RWKV_WIDTH, D_MODEL, D_MODEL)
IN_WIDTH = 2 * ATT_WIDTH + 2 * ATT_KV_WIDTH + 4 * RWKV_WIDTH + DECAY_LORA + ICLR_LORA + 2 * D_MODEL

kernel_name = 'hybrid_swa_sink_rwkv7_gated_merge'


def _offsets(sizes):
    out, acc = [], 0
    for s in sizes[:-1]:
        acc += s
        out.append(acc)
    return out


def _rmsnorm(x, w):
    xf = x.astype(jnp.float32)
    y = xf * lax.rsqrt(jnp.mean(xf * xf, axis=-1, keepdims=True) + RMS_EPS)
    return (y * w.astype(jnp.float32)).astype(x.dtype)


def _rope(x, pos):
    half = x.shape[-1] // 2
    inv = 1.0 / (ROPE_THETA ** (jnp.arange(half, dtype=jnp.float32) / half))
    ang = pos[:, None] * inv[None, :]
    cos = jnp.cos(ang)[:, None, :]
    sin = jnp.sin(ang)[:, None, :]
    xf = x.astype(jnp.float32)
    x1, x2 = xf[..., :half], xf[..., half:]
    return jnp.concatenate([x1 * cos - x2 * sin, x2 * cos + x1 * sin], axis=-1).astype(x.dtype)


def _token_shift(z):
    return jnp.pad(z, ((0, 0), (1, 0), (0, 0)))[:, :-1]


def _sliding_window_gqa(q, k, v, sinks):
    B, T = q.shape[0], q.shape[1]
    pad = (-T) % BLOCK
    Tp = T + pad
    nb = Tp // BLOCK
    padw = ((0, 0), (pad, 0), (0, 0), (0, 0))
    q = jnp.pad(q, padw).reshape(B, nb, BLOCK, ATT_KV_HEADS, ATT_GROUP, HEAD_DIM)
    k = jnp.pad(k, padw).reshape(B, nb, BLOCK, ATT_KV_HEADS, HEAD_DIM)
    v = jnp.pad(v, padw).reshape(B, nb, BLOCK, ATT_KV_HEADS, HEAD_DIM)

    def window(t):
        prev = jnp.concatenate([jnp.zeros_like(t[:, :1]), t[:, :-1]], axis=1)
        return jnp.moveaxis(jnp.concatenate([prev, t], axis=2), 1, 0)

    kw, vw = window(k), window(v)
    qm = jnp.moveaxis(q, 1, 0)
    sink = sinks.astype(jnp.float32).reshape(ATT_KV_HEADS, ATT_GROUP)[None, :, :, None, None]
    scale = HEAD_DIM ** -0.5

    def block(args):
        n, qn, kn, vn = args
        s = jnp.einsum('bqhgd,bkhd->bhgqk', qn, kn).astype(jnp.float32) * scale
        qi = n * BLOCK + jnp.arange(BLOCK)
        kj = (n - 1) * BLOCK + jnp.arange(2 * BLOCK)
        diff = qi[:, None] - kj[None, :]
        ok = (diff >= 0) & (diff < WINDOW) & (kj[None, :] >= pad)
        s = jnp.where(ok, s, NEG_INF)
        sb = jnp.broadcast_to(sink, s.shape[:-1] + (1,))
        p = jax.nn.softmax(jnp.concatenate([s, sb], axis=-1), axis=-1)[..., :-1]
        return jnp.einsum('bhgqk,bkhd->bqhgd', p.astype(vn.dtype), vn)

    o = lax.map(block, (jnp.arange(nb), qm, kw, vw))
    o = jnp.moveaxis(o, 0, 1).reshape(B, Tp, ATT_WIDTH)
    return o[:, pad:]


def _rwkv7_time_mix(r, k, v, w_lo, a_lo, mu, w0, w2, a0, a2, k_k, k_a, r_k, ln_w, ln_b):
    f32 = jnp.float32
    B, T = r.shape[0], r.shape[1]
    z = jnp.concatenate([r, k, v, w_lo, a_lo], axis=-1).astype(f32)
    z = z + (_token_shift(z) - z) * mu.astype(f32)
    r, k, v, w_lo, a_lo = jnp.split(z, [RWKV_WIDTH, 2 * RWKV_WIDTH, 3 * RWKV_WIDTH, 3 * RWKV_WIDTH + DECAY_LORA], axis=-1)
    w = -jax.nn.softplus(-(w0.astype(f32) + jnp.tanh(w_lo) @ w2.astype(f32))) - 0.5
    decay = jnp.exp(-jnp.exp(w))
    a = jax.nn.sigmoid(a0.astype(f32) + a_lo @ a2.astype(f32))

    def hs(t):
        return t.reshape(B, T, RWKV_HEADS, RWKV_HEAD)

    kk = hs(k * k_k.astype(f32))
    kk = kk / jnp.maximum(jnp.sqrt(jnp.sum(kk * kk, axis=-1, keepdims=True)), 1e-12)
    k = k * (1.0 + (a - 1.0) * k_a.astype(f32))
    r, k, v, decay, a = hs(r), hs(k), hs(v), hs(decay), hs(a)
    xs = tuple(jnp.moveaxis(t, 1, 0) for t in (r, decay, k, v, -kk, kk * a))

    def step(S, inp):
        r_t, w_t, k_t, v_t, a_t, b_t = inp
        sa = jnp.einsum('bhij,bhj->bhi', S, a_t)
        S = S * w_t[:, :, None, :] + sa[..., None] * b_t[:, :, None, :] + v_t[..., None] * k_t[:, :, None, :]
        return S, jnp.einsum('bhij,bhj->bhi', S, r_t)

    S0 = jnp.zeros((B, RWKV_HEADS, RWKV_HEAD, RWKV_HEAD), f32)
    _, y = lax.scan(step, S0, xs)
    y = jnp.moveaxis(y, 0, 1)
    mean = jnp.mean(y, axis=-1, keepdims=True)
    var = jnp.mean(jnp.square(y - mean), axis=-1, keepdims=True)
    y = ((y - mean) * lax.rsqrt(var + GN_EPS)).reshape(B, T, RWKV_WIDTH) * ln_w.astype(f32) + ln_b.astype(f32)
    bonus = jnp.sum(r * k * r_k.astype(f32), axis=-1, keepdims=True) * v
    return y + bonus.reshape(B, T, RWKV_WIDTH)


def setup_inputs(seed: int = 0) -> dict:
    key = jax.random.key(seed)
    ks = jax.random.split(key, 20)
    f32 = jnp.float32

    def nrm(k, shape, s):
        return jax.random.normal(k, shape, f32) * s

    L = DEPTH
    return {
        'x': nrm(ks[0], (BATCH, SEQ, D_MODEL), 1.0),
        'meta_tokens': nrm(ks[1], (N_META, D_MODEL), 1.0),
        'norm_w': 1.0 + nrm(ks[2], (L, D_MODEL), 0.05),
        'w_in': nrm(ks[3], (L, D_MODEL, IN_WIDTH), D_MODEL ** -0.5),
        'att_sinks': nrm(ks[4], (L, ATT_Q_HEADS), 1.0),
        'rk_mu': jax.random.uniform(ks[5], (L, RWKV_SHIFT_WIDTH), f32, 0.0, 1.0),
        'rk_w0': jax.random.uniform(ks[6], (L, RWKV_WIDTH), f32, -6.0, -1.0),
        'rk_w2': nrm(ks[7], (L, DECAY_LORA, RWKV_WIDTH), 0.1),
        'rk_a0': nrm(ks[8], (L, RWKV_WIDTH), 0.5),
        'rk_a2': nrm(ks[9], (L, ICLR_LORA, RWKV_WIDTH), 0.5 * ICLR_LORA ** -0.5),
        'rk_k_k': 0.85 + nrm(ks[10], (L, RWKV_WIDTH), 0.05),
        'rk_k_a': 1.0 + nrm(ks[11], (L, RWKV_WIDTH), 0.05),
        'rk_r_k': nrm(ks[12], (L, RWKV_HEADS, RWKV_HEAD), 0.1),
        'rk_ln_w': 1.0 + nrm(ks[13], (L, RWKV_WIDTH), 0.05),
        'rk_ln_b': nrm(ks[14], (L, RWKV_WIDTH), 0.01),
        'w_branch_att': nrm(ks[15], (L, ATT_WIDTH, D_MODEL), ATT_WIDTH ** -0.5),
        'w_branch_rwkv': nrm(ks[16], (L, RWKV_WIDTH, D_MODEL), RWKV_WIDTH ** -0.5),
        'w_out': nrm(ks[17], (L, D_MODEL, D_MODEL), D_MODEL ** -0.5),
        'final_norm_w': 1.0 + nrm(ks[18], (D_MODEL,), 0.05),
    }


def reference(x, meta_tokens, norm_w, w_in, att_sinks, rk_mu, rk_w0, rk_w2, rk_a0, rk_a2, rk_k_k, rk_k_a, rk_r_k, rk_ln_w, rk_ln_b, w_branch_att, w_branch_rwkv, w_out, final_norm_w):
    B = x.shape[0]
    meta = jnp.broadcast_to(meta_tokens.astype(x.dtype)[None], (B, N_META, D_MODEL))
    h = jnp.concatenate([meta, x], axis=1)
    T = h.shape[1]
    pos = jnp.arange(T, dtype=jnp.float32)
    split_at = _offsets(SPLIT_SIZES)
    for l in range(DEPTH):
        u = _rmsnorm(h, norm_w[l])
        p = u @ w_in[l].astype(u.dtype)
        q, ka, va, ga, r, kr, vr, wl, al, gr, ma, mr = jnp.split(p, split_at, axis=-1)
        q = _rope(q.reshape(B, T, ATT_Q_HEADS, HEAD_DIM), pos)
        ka = _rope(ka.reshape(B, T, ATT_KV_HEADS, HEAD_DIM), pos)
        va = va.reshape(B, T, ATT_KV_HEADS, HEAD_DIM)
        att = _sliding_window_gqa(q, ka, va, att_sinks[l])
        rw = _rwkv7_time_mix(r, kr, vr, wl, al, rk_mu[l], rk_w0[l], rk_w2[l], rk_a0[l], rk_a2[l], rk_k_k[l], rk_k_a[l], rk_r_k[l], rk_ln_w[l], rk_ln_b[l]).astype(h.dtype)
        ya = (att * jax.nn.silu(ga)) @ w_branch_att[l].astype(h.dtype)
        yr = (rw * jax.nn.silu(gr)) @ w_branch_rwkv[l].astype(h.dtype)
        merged = jax.nn.sigmoid(ma) * ya + jax.nn.sigmoid(mr) * yr
        h = h + merged @ w_out[l].astype(h.dtype)
    y = _rmsnorm(h, final_norm_w)
    return y[:, N_META:]
```

```python
import contextlib
import numpy as np
import concourse.bass as bass
import concourse.mybir as mybir
from concourse.bass_utils import run_bass_kernel_spmd

F32 = mybir.dt.float32
BF16 = mybir.dt.bfloat16
AF = mybir.ActivationFunctionType
ALU = mybir.AluOpType
AX = mybir.AxisListType

D = 1024
NIN = 8576
NTM = 6528
RW0 = 2304
RWN = 3200
GR0 = 5504
N_META = 16
PAD = 112
RMS_EPS = 1e-6
GN_EPS = 64e-5
NCONST = 128 + 5 * 512 + 512


class Buf:
    def __init__(self, name):
        self.name = name
        self.w = {}
        self.r = {}
        self.dsem = None
        self.dkey = None
        self.dcnt = 0
        self.subs = None

    def split(self, n):
        self.subs = [Buf("%s_s%d" % (self.name, i)) for i in range(n)]
        for sbuf in self.subs:
            sbuf.w = dict(self.w)
            sbuf.r = dict(self.r)
        return self


def expand(lst):
    out = []
    for b in lst:
        if isinstance(b, tuple):
            out.append(b[0].subs[b[1]] if b[0].subs else b[0])
        elif b.subs:
            out.extend(b.subs)
        else:
            out.append(b)
    return out


class Eng:
    def __init__(self, key, getter):
        self.key = key
        self.getter = getter
        self.cnt = 0
        self.seen = {}
        self.ops = []
        self.sem = None


class Prog:
    def __init__(self, nc, stack):
        self.nc = nc
        self.stack = stack
        self.sems = {}
        self.E = {}
        for key in ("pe", "act", "dve", "pool", "sp"):
            e = Eng(key, None)
            e.sem = stack.enter_context(nc.semaphore("s_" + key))
            self.sems[key] = e.sem
            self.E[key] = e
        self.nsem = 5

    def _deps(self, E, reads, writes, acc, waw):
        deps = {}

        def add(k, v):
            if deps.get(k, 0) < v:
                deps[k] = v
        for b in reads:
            for k, v in b.w.items():
                add(k, v)
            if getattr(b, "is_bank", False):
                for k, v in b.r.items():
                    if k != E.key:
                        add(k, v)
        for b in writes:
            if waw:
                for k, v in b.w.items():
                    if acc and k == E.key:
                        continue
                    add(k, v)
            for k, v in b.r.items():
                add(k, v)
        for k, v in deps.items():
            if E.seen.get(k, 0) < v:
                E.seen[k] = v
                sem = self.sems[k]
                E.ops.append(lambda e, sem=sem, v=v: e.wait_ge(sem, v))

    def op(self, ek, fn, reads=(), writes=(), acc=False, waw=True):
        E = self.E[ek]
        reads = expand(reads)
        writes = expand(writes)
        if ek != "pe":
            for b in reads:
                if getattr(b, "held", False):
                    b.held = False
        self._deps(E, reads, writes, acc, waw)
        E.cnt += 1
        cnt = E.cnt
        sem = E.sem
        E.ops.append(lambda e, fn=fn, sem=sem: fn(e).then_inc(sem, 1))
        for b in reads:
            b.r[E.key] = cnt
        for b in writes:
            b.w[E.key] = cnt

    def dma(self, out_ap, in_ap, sembuf, reads=(), writes=(), qk="sp"):
        Q = self.E[qk]
        reads = expand(reads)
        writes = expand(writes)
        self._deps(Q, reads, writes, False, True)
        sb = sembuf
        if sb.dsem is None:
            sb.dkey = "d_" + sb.name
            sb.dsem = self.stack.enter_context(self.nc.semaphore(sb.dkey))
            self.sems[sb.dkey] = sb.dsem
            self.nsem += 1
        if sb.dcnt > 0 and Q.seen.get(sb.dkey, 0) < sb.dcnt:
            Q.seen[sb.dkey] = sb.dcnt
            Q.ops.append(lambda e, sem=sb.dsem, v=sb.dcnt: e.wait_ge(sem, v))
        sb.dcnt += 16
        val = sb.dcnt
        dsem = sb.dsem
        Q.ops.append(lambda e, o=out_ap, i=in_ap, dsem=dsem: e.dma_start(out=o, in_=i).then_inc(dsem, 16))
        for b in reads:
            b.r[sb.dkey] = val
        for b in writes:
            b.w[sb.dkey] = val

    def wait_all(self, ek, bufs):
        E = self.E[ek]
        for b in expand(bufs):
            for k, v in list(b.w.items()) + list(b.r.items()):
                if E.seen.get(k, 0) < v:
                    E.seen[k] = v
                    sem = self.sems[k]
                    E.ops.append(lambda e, sem=sem, v=v: e.wait_ge(sem, v))


def bc_last(ap, n):
    return bass.AP(ap.tensor, ap.offset, [list(x) for x in ap.ap] + [[0, n]])


def bc_mid(ap, n):
    a = [list(x) for x in ap.ap]
    return bass.AP(ap.tensor, ap.offset, [a[0], [0, n]] + a[1:])


def build(NBLK):
    nc = bass.Bass("TRN2", target_bir_lowering=False)
    NOUT = NBLK - 1

    def din(name, shape):
        return nc.dram_tensor(name, list(shape), F32, kind="ExternalInput").ap()
    h0 = din("h0", [NBLK * 128, D])
    w_in = din("w_in", [D, NIN])
    w_ba = din("w_ba", [D, D])
    w_br = din("w_br", [D, D])
    w_out = din("w_out", [D, D])
    consts = din("consts", [128, NCONST])
    tabs = din("tabs", [NBLK * 128, 64])
    pvec = {}
    for nm, n in (("normw", D), ("fnw", D), ("mu", RWN), ("w0", D), ("a0", D), ("kk", D), ("ka", D),
                  ("rk", D), ("lnw", D), ("lnb", D), ("sinks", 16)):
        pvec[nm] = din("p_" + nm, [1, n])
    w2d = din("w2", [64, D])
    a2d = din("a2", [64, D])
    y = nc.dram_tensor("y", [NOUT * 128, D], F32, kind="ExternalOutput").ap()
    win_bf = nc.dram_tensor("win_bf", [8, 128, NIN], BF16, kind="Internal").ap()
    wba_bf = nc.dram_tensor("wba_bf", [8, 128, D], BF16, kind="Internal").ap()
    wbr_bf = nc.dram_tensor("wbr_bf", [8, 128, D], BF16, kind="Internal").ap()
    wout_bf = nc.dram_tensor("wout_bf", [8, 128, D], BF16, kind="Internal").ap()

    with contextlib.ExitStack() as stack:
        P = Prog(nc, stack)
        bufs = {}

        def sb(name, shape, dt=F32):
            t = stack.enter_context(nc.sbuf_tensor(name, list(shape), dt))
            b = Buf(name)
            bufs[name] = b
            return t, b

        def B(name):
            b = Buf(name)
            return b

        cstb, Bcstb = sb("cstb", [128, NCONST], BF16)
        cstf, Bcstf = sb("cstf", [128, 512])
        pslot = [sb("pslot%d" % i, [128, D]) for i in range(3)]
        esk, Besk = sb("esk", [128, 16])
        w2b, Bw2b = sb("w2b", [64, 2 * D], BF16)
        xs = [sb("xs%d" % i, [128, D]) for i in range(2)]
        tab = [sb("tab%d" % i, [128, 64]) for i in range(2)]
        NW = 4
        wbuf = [sb("wbuf%d" % i, [128, 8, 512], BF16) for i in range(NW)]
        u_bf, Bu = sb("u_bf", [128, D], BF16)
        uTs2 = [sb("uT%d" % i, [128, D], BF16) for i in range(2)]
        uTs_t, BuTs = sb("uTs", [128, D], BF16)
        ulast, Bulast = sb("ulast", [128, 8], BF16)
        junkH, BjunkH = sb("junkH", [128, D])
        shtmp = [sb("shtmp%d" % i, [128, 512]) for i in range(2)]
        Bsm_fn = B("sm_fn")
        p_sb, Bp_all = sb("p_sb", [128, NTM])
        Bp_att, Bp_rw, Bp_gr = B("p_att"), B("p_rw"), B("p_gr")
        small, Bsmall = sb("small", [128, 96])
        Bsm_att, Bsm_kk, Bsm_gn, Bsm_bon = B("sm_att"), B("sm_kk"), B("sm_gn"), B("sm_bon")
        tA, BtA = sb("tA", [128, D])
        tB, BtB = sb("tB", [128, D])
        tC, BtC = sb("tC", [128, D])
        tD, BtD = sb("tD", [128, D])
        tE, BtE = sb("tE", [128, D])
        tF, BtF = sb("tF", [128, D])
        tG, BtG = sb("tG", [128, D])
        tH, BtH = sb("tH", [128, D])
        q_rot, Bq_rot = sb("q_rot", [128, D], BF16)
        k_rot, Bk_rot = sb("k_rot", [128, 128], BF16)
        qT, BqT = sb("qT", [64, 16 * 128], BF16)
        kT = [sb("kT%d" % i, [64, 256], BF16) for i in range(2)]
        vaug = [sb("vaug%d" % i, [128, 2, 65], BF16) for i in range(2)]
        pT = [sb("pT%d" % i, [128, 512], BF16) for i in range(2)]
        y_sb, By = sb("y_sb", [128, D])
        att, Batt = p_sb, Bp_att
        gatt, Bgatt = sb("gatt", [128, D], BF16)
        grw, Bgrw = sb("grw", [128, D], BF16)
        gattT, BgattT = sb("gattT", [128, D], BF16)
        grwT, BgrwT = sb("grwT", [128, D], BF16)
        mrgT, BmrgT = sb("mrgT", [128, D], BF16)
        lora_bf, Blora = sb("lora_bf", [128, 128], BF16)
        loraT, BloraT = sb("loraT", [64, 256], BF16)
        sig, Bsig = sb("sig", [128, D])
        t1, Bt1 = sb("t1", [128, D])
        junk, Bjunk = t1, Bt1
        tl = {"r_t": q_rot, "a_t": gattT, "b_t": grwT, "k_t": mrgT}
        Btl = {"r_t": Bq_rot, "a_t": BgattT, "b_t": BgrwT, "k_t": BmrgT}
        for nm in ("b_h", "k_h", "v_b"):
            tl[nm], Btl[nm] = sb("tl_" + nm, [128, D], BF16)
        fm = {"rT": sig.bitcast(BF16)[0:64, :], "aT": t1.bitcast(BF16)[0:64, :],
              "bT": tD.bitcast(BF16)[0:64, :], "kT": tH.bitcast(BF16)[0:64, :]}
        Bfm = {"rT": Bsig, "aT": Bt1, "bT": BtD, "kT": BtH}
        NS = 2
        mats = []
        for s_ in range(NS):
            d_ = {}
            for nm in ("N0", "N1", "L0", "L1", "AK", "RB", "RK"):
                d_[nm] = sb("m%d_%s" % (s_, nm), [128, 512], BF16)
            d_["Xf"] = sb("m%d_Xf" % s_, [128, 256])
            d_["Xb"] = sb("m%d_Xb" % s_, [128, 256], BF16)
            mats.append(d_)
        Hf, _ = sb("Hf", [64, D])
        Hb, _ = sb("Hb", [64, D], BF16)
        BHf = [B("Hf%d" % i) for i in range(4)]
        BHb = [B("Hb%d" % i) for i in range(4)]
        tmpH, BtmpH = sb("tmpH", [64, 256])
        pcfm, Bpcfm = sb("pcfm", [64, 16])
        osb = [sb("osb%d" % i, [128, D]) for i in range(1)]
        carry = nc.dram_tensor("carry", [2, RWN], F32, kind="Internal").ap()
        BCARRY = [B("carry0"), B("carry1")]
        Bps0 = B("ps_row0")
        cst = p_sb[:, 0:NCONST]
        Bcst = Bp_all
        w2f = p_sb[0:64, NCONST:NCONST + 2 * D]
        Bw2f = B("w2f")
        stg_f = [(tB, B("stgf0")), (tC, B("stgf1"))]
        sigb = sig.bitcast(BF16)
        stg_b = [(sigb[:, 0:1024], B("stgb0")), (tA.bitcast(BF16)[:, 0:1024], B("stgb1"))]

        for bb_ in (BtA, BtB, BtC, BtD, BtE, BtF, BtG, BtH, Bq_rot, BgattT, BgrwT, BmrgT, Btl["b_h"], Btl["k_h"],
                    Btl["v_b"], Bsig, Bt1, Bpcfm):
            bb_.split(2)
        Bsm_kk2 = [B("sm_kk0"), B("sm_kk1")]

        banks = []
        for i in range(8):
            t = stack.enter_context(nc.psum_tensor("bank%d" % i, [128, 512], F32))
            bb_ = B("bank%d" % i)
            bb_.is_bank = True
            banks.append((t, bb_))
        bank_i = [0]

        def nb():
            for _ in range(8):
                t = banks[bank_i[0] % 8]
                bank_i[0] += 1
                if not getattr(t[1], "held", False):
                    t[1].held = True
                    return t
            raise RuntimeError("no free PSUM bank")

        BWIN, BWBA, BWBR, BWOUT = B("win"), B("wba"), B("wbr"), B("wout")
        BY = B("yout")

        ident = cstb[:, 0:128]
        maskA = cstb[:, 128:640]
        maskA1 = cstb[:, 640:1152]
        SU4 = cstb[:, 1152:1664]
        IU4 = cstb[:, 1664:2176]
        SL4 = cstb[:, 2176:2688]
        tri_f = cstf[:, 0:128]
        ones_f = cstf[:, 128:256]
        sh_f = cstf[:, 256:384]
        e_f = cstf[:, 384:512]

        def mm(out, lhsT, rhs, start, stop, reads, writes):
            P.op("pe", lambda e: e.matmul(out, lhsT, rhs, start=start, stop=stop), reads=reads, writes=writes,
                 acc=True)

        def tr(out, in_, reads, writes, np_=128):
            P.op("pe", lambda e: e.transpose(out, in_, ident[0:np_, 0:np_]), reads=list(reads) + [Bcstb],
                 writes=writes, acc=True)

        def act(out, in_, func, reads, writes, bias=None, scale=None, waw=True):
            kw = {}
            if bias is not None:
                kw["bias"] = bias
            if scale is not None:
                kw["scale"] = scale
            P.op("act", lambda e: e.activation(out, in_, func, **kw), reads=reads, writes=writes, waw=waw)

        def tt(ek, out, in0, in1, op, reads, writes, waw=True):
            P.op(ek, lambda e: e.tensor_tensor(out, in0, in1, op), reads=reads, writes=writes, waw=waw)

        def ts(ek, out, in0, s1, s2, op0, op1, reads, writes):
            if op1 is None:
                P.op(ek, lambda e: e.tensor_scalar(out, in0, s1, None, op0), reads=reads, writes=writes)
            else:
                P.op(ek, lambda e: e.tensor_scalar(out, in0, s1, s2, op0, op1), reads=reads, writes=writes)

        def stt(out, in0, scalar, in1, op0, op1, reads, writes):
            P.op("dve", lambda e: e.scalar_tensor_tensor(out, in0, scalar, in1, op0, op1), reads=reads,
                 writes=writes)

        def red(out, in_, reads, writes):
            P.op("dve", lambda e: e.tensor_reduce(out, in_, AX.X, ALU.add), reads=reads, writes=writes)

        def recip(out, in_, reads, writes):
            P.op("dve", lambda e: e.reciprocal(out, in_), reads=reads, writes=writes)

        def cp(ek, out, in_, reads, writes, waw=True):
            if ek == "act":
                act(out, in_, AF.Copy, reads, writes, waw=waw)
            else:
                P.op(ek, lambda e: e.tensor_copy(out, in_), reads=reads, writes=writes, waw=waw)

        def memset(ek, ap, val, writes):
            P.op(ek, lambda e: e.memset(ap, val), writes=writes)

        pcnt = [0]

        def prm(nm, c0=0, n=D):
            t, Bt = pslot[pcnt[0] % 3]
            pcnt[0] += 1
            src = bass.AP(pvec[nm].tensor, c0, [[0, 128], [1, n]])
            P.dma(t[:, 0:n], src, Bt, writes=[Bt], qk="pool")
            return t[:, 0:n], Bt

        def strided_cols(ap1, step, n):
            a = [list(x) for x in ap1.ap]
            return bass.AP(ap1.tensor, ap1.offset, [a[0], [step, n]])

        def merge(dst, src):
            for d1 in expand([dst]):
                for dd, sd in ((d1.r, src.w), (d1.r, src.r)):
                    for k, v in sd.items():
                        if dd.get(k, 0) < v:
                            dd[k] = v

        P.dma(cst, consts[:, :], Bcst, writes=[Bcst])
        cp("dve", cstb[:], cst, [Bcst], [Bcstb])
        cp("pool", cstf[:], p_sb[:, 2688:3200], [Bcst], [Bcstf])
        P.dma(esk[:], bass.AP(pvec["sinks"].tensor, 0, [[0, 128], [1, 16]]), Besk, writes=[Besk])
        act(esk[:], esk[:], AF.Exp, [Besk], [Besk])
        P.dma(w2f[:, 0:D], w2d[:, :], Bw2f, writes=[Bw2f])
        P.dma(w2f[:, D:2 * D], a2d[:, :], B("w2f_b"), writes=[Bw2f])
        cp("dve", w2b[:], w2f, [Bw2f], [Bw2b])
        si = 0
        cast_engs = ["dve", "pool", "act"]
        for (wsrc, wdst, Bd, ncol) in ((w_in, win_bf, BWIN, NIN), (w_ba, wba_bf, BWBA, D), (w_br, wbr_bf, BWBR, D),
                                       (w_out, wout_bf, BWOUT, D)):
            pw = 536 if ncol == NIN else 1024
            for c in range(8):
                for c0 in range(0, ncol, pw):
                    sf, Bsf = stg_f[si % 2]
                    sbf, Bsb = stg_b[si % 2]
                    P.dma(sf[:, 0:pw], wsrc[c * 128:(c + 1) * 128, c0:c0 + pw], Bsf, writes=[Bsf])
                    cp(cast_engs[si % 3], sbf[:, 0:pw], sf[:, 0:pw], [Bsf], [Bsb])
                    P.dma(wdst[c, :, c0:c0 + pw], sbf[:, 0:pw], Bsb, reads=[Bsb], writes=[Bd])
                    si += 1
        merge(BtB, stg_f[0][1])
        merge(BtC, stg_f[1][1])
        merge(Bsig, stg_b[0][1])
        merge(BtA, stg_b[1][1])
        for bb in (Bp_att, Bp_rw, Bp_gr):
            merge(bb, Bcst)
            merge(bb, Bw2f)
        memset("pool", ulast[:], 0.0, [Bulast])
        memset("pool", Hf[:], 0.0, BHf)
        memset("pool", Hb[:], 0.0, BHb)
        for i in range(2):
            memset("pool", vaug[i][0][:], 1.0, [vaug[i][1]])

        def load_x(b):
            par = b % 2
            P.dma(xs[par][0][:], h0[b * 128:(b + 1) * 128, :], xs[par][1], writes=[xs[par][1]])
            P.dma(tab[par][0][:], tabs[b * 128:(b + 1) * 128, :], tab[par][1], writes=[tab[par][1]])

        load_x(0)
        if NBLK > 1:
            load_x(1)
        wcnt = [0]

        def load_w(src, Bsrc, c0, ncols):
            wb, Bwb = wbuf[wcnt[0] % NW]
            wcnt[0] += 1
            P.dma(wb[:, :, 0:ncols], src[:, :, c0:c0 + ncols].rearrange("c p n -> p c n"), Bwb, reads=[Bsrc],
                  writes=[Bwb])
            return wb, Bwb

        def rmsnorm_scale(src, Bsrc, wrep, Bwrep, out, Bout, jk=None, Bjk=None, so=0, Bs=None):
            jk = junk if jk is None else jk
            Bjk = Bjunk if Bjk is None else Bjk
            Bs = Bsmall if Bs is None else Bs
            act(jk[:], src, AF.Square, [Bsrc], [Bjk])
            red(small[:, so:so + 1], jk[:], [Bjk], [Bs])
            ts("dve", small[:, so + 1:so + 2], small[:, so:so + 1], 1.0 / D, RMS_EPS, ALU.mult, ALU.add, [Bs], [Bs])
            act(small[:, so + 2:so + 3], small[:, so + 1:so + 2], AF.Sqrt, [Bs], [Bs])
            recip(small[:, so + 3:so + 4], small[:, so + 2:so + 3], [Bs], [Bs])
            stt(out, src, small[:, so + 3:so + 4], wrep, ALU.mult, ALU.mult, [Bsrc, Bs, Bwrep], [Bout])

        def head_gen(b):
            par = b % 2
            x_t, Bx = xs[par]
            uT, BuT = uTs2[par]
            nw_ap, Bnw = prm("normw")
            rmsnorm_scale(x_t[:], Bx, nw_ap, Bnw, u_bf[:], Bu, jk=junkH, Bjk=BjunkH)
            bk, Bbk = nb()
            bkb = bk.bitcast(BF16)
            for c in range(8):
                tr(bkb[:, c * 128:(c + 1) * 128], u_bf[:, c * 128:(c + 1) * 128], [Bu], [Bbk])
            cp("act", uT[:], bkb[:, 0:1024], [Bbk], [BuT])
            uT3 = uT[:].rearrange("p (c t) -> p c t", c=8, t=128)
            uS3 = uTs_t[:].rearrange("p (c t) -> p c t", c=8, t=128)
            cp("pool", uS3[:, :, 1:128], uT3[:, :, 0:127], [BuT], [BuTs])
            cp("pool", uS3[:, :, 0:1], ulast[:].rearrange("p (c o) -> p c o", c=8, o=1), [Bulast], [BuTs], waw=False)
            cp("pool", ulast[:].rearrange("p (c o) -> p c o", c=8, o=1), uT3[:, :, 127:128], [BuT], [Bulast])
            yield
            for nt in range(13):
                n0 = nt * 512
                nsz = min(512, NTM - n0)
                wb, Bwb = load_w(win_bf, BWIN, n0, nsz)
                bk, Bbk = nb()
                for c in range(8):
                    mm(bk[:, 0:nsz], uT[:, c * 128:(c + 1) * 128], wb[:, c, 0:nsz], c == 0, c == 7,
                       [BuT, Bwb], [Bbk])
                r0, r1 = max(n0, RW0), min(n0 + nsz, GR0)
                if r1 > r0:
                    bk2, Bbk2 = nb()
                    for c in range(8):
                        mm(bk2[:, r0 - n0:r1 - n0], uTs_t[:, c * 128:(c + 1) * 128], wb[:, c, r0 - n0:r1 - n0],
                           c == 0, c == 7, [BuTs, Bwb], [Bbk2])
                segs = ((0, 1280, False, Bp_att), (1280, RW0, True, Bp_att), (RW0, GR0, False, Bp_rw),
                        (GR0, NTM, True, Bp_gr))
                has_silu = any(sg[2] and sg[0] < n0 + nsz and sg[1] > n0 for sg in segs)
                for (s0, s1, is_silu, Bseg) in segs:
                    a0, a1 = max(n0, s0), min(n0 + nsz, s1)
                    if a1 <= a0:
                        continue
                    if is_silu:
                        act(p_sb[:, a0:a1], bk[:, a0 - n0:a1 - n0], AF.Silu, [Bbk], [Bseg], waw=False)
                    else:
                        eng = "act" if (has_silu or nt % 2 == 0) else "dve"
                        cp(eng, p_sb[:, a0:a1], bk[:, a0 - n0:a1 - n0], [Bbk], [Bseg], waw=False)
                if r1 > r0:
                    n_ = r1 - r0
                    st_t, Bst = shtmp[nt % 2]
                    mu_ap, Bmu = prm("mu", r0 - RW0, n_)
                    tt("dve", st_t[:, 0:n_], bk2[:, r0 - n0:r1 - n0], p_sb[:, r0:r1], ALU.subtract,
                       [Bbk2, Bp_rw], [Bst])
                    tt("pool", st_t[:, 0:n_], st_t[:, 0:n_], mu_ap, ALU.mult, [Bst, Bmu], [Bst])
                    tt("dve" if nt % 2 == 0 else "pool", p_sb[:, r0:r1], st_t[:, 0:n_], p_sb[:, r0:r1], ALU.add,
                       [Bst, Bp_rw], [Bp_rw], waw=False)
                yield

        def mid(b):
            par = b % 2
            x_t, Bx = xs[par]
            tb_t, Btb = tab[par]
            uT, BuT = uTs2[par]
            if b >= 1 and b + 1 < NBLK:
                load_x(b + 1)

            def v3(ap, a, bdim):
                return ap.rearrange("p (a b) -> p a b", a=a, b=bdim)
            q3 = v3(p_sb[:, 0:1024], 16, 64)
            qr3 = v3(q_rot[:], 16, 64)
            cosb = bc_mid(tb_t[:, 0:32], 16)
            sinb = bc_mid(tb_t[:, 32:64], 16)
            A3 = v3(tA[:, 0:512], 16, 32)
            B3 = v3(tB[:, 0:512], 16, 32)
            C3 = v3(tA[:, 512:1024], 16, 32)
            D3 = v3(tB[:, 512:1024], 16, 32)
            tt("dve", A3, q3[:, :, 0:32], cosb, ALU.mult, [Bp_att, Btb], [BtA])
            tt("pool", B3, q3[:, :, 32:64], sinb, ALU.mult, [Bp_att, Btb], [BtB])
            tt("dve", qr3[:, :, 0:32], A3, B3, ALU.subtract, [BtA, BtB], [Bq_rot])
            tt("pool", C3, q3[:, :, 32:64], cosb, ALU.mult, [Bp_att, Btb], [BtA])
            tt("dve", D3, q3[:, :, 0:32], sinb, ALU.mult, [Bp_att, Btb], [BtB])
            tt("pool", qr3[:, :, 32:64], C3, D3, ALU.add, [BtA, BtB], [Bq_rot])
            k3 = v3(p_sb[:, 1024:1152], 2, 64)
            kr3 = v3(k_rot[:], 2, 64)
            cos2 = bc_mid(tb_t[:, 0:32], 2)
            sin2 = bc_mid(tb_t[:, 32:64], 2)
            E3 = v3(tC[:, 0:64], 2, 32)
            F3 = v3(tC[:, 64:128], 2, 32)
            G3 = v3(tC[:, 128:192], 2, 32)
            H3 = v3(tC[:, 192:256], 2, 32)
            tt("dve", E3, k3[:, :, 0:32], cos2, ALU.mult, [Bp_att, Btb], [BtC])
            tt("dve", F3, k3[:, :, 32:64], sin2, ALU.mult, [Bp_att, Btb], [BtC])
            tt("dve", kr3[:, :, 0:32], E3, F3, ALU.subtract, [BtC], [Bk_rot])
            tt("dve", G3, k3[:, :, 32:64], cos2, ALU.mult, [Bp_att, Btb], [BtC])
            tt("dve", H3, k3[:, :, 0:32], sin2, ALU.mult, [Bp_att, Btb], [BtC])
            tt("dve", kr3[:, :, 32:64], G3, H3, ALU.add, [BtC], [Bk_rot])
            va_t, Bva = vaug[par]
            cp("pool", va_t[:, :, 0:64], v3(p_sb[:, 1152:1280], 2, 64), [Bp_att], [Bva])
            kT_t, BkT = kT[par]
            bk, Bbk = nb()
            bkb = bk.bitcast(BF16)
            for g in range(2):
                tr(bkb[0:64, g * 128:(g + 1) * 128], k_rot[:, g * 64:(g + 1) * 64], [Bk_rot], [Bbk])
            cp("dve", kT_t[:], bkb[0:64, 0:256], [Bbk], [BkT])
            if b >= 1:
                for hh in range(2):
                    bk, Bbk = nb()
                    bkb = bk.bitcast(BF16)
                    for i in range(8):
                        h = hh * 8 + i
                        tr(bkb[0:64, i * 128:(i + 1) * 128], q_rot[:, h * 64:(h + 1) * 64], [Bq_rot], [Bbk])
                    cp("act" if hh == 0 else "dve", qT[:, hh * 1024:(hh + 1) * 1024], bkb[0:64, 0:1024], [Bbk],
                       [BqT], waw=False)

            r_ = p_sb[:, RW0:RW0 + 1024]
            k_ = p_sb[:, RW0 + 1024:RW0 + 2048]
            v_ = p_sb[:, RW0 + 2048:RW0 + 3072]
            wl_ = p_sb[:, RW0 + 3072:RW0 + 3136]
            al_ = p_sb[:, RW0 + 3136:RW0 + 3200]
            act(lora_bf[:, 0:64], wl_, AF.Tanh, [Bp_rw], [Blora])
            cp("dve", lora_bf[:, 64:128], al_, [Bp_rw], [Blora])
            bk, Bbk = nb()
            bkb = bk.bitcast(BF16)
            tr(bkb[0:64, 0:128], lora_bf[:, 0:64], [Blora], [Bbk])
            tr(bkb[0:64, 128:256], lora_bf[:, 64:128], [Blora], [Bbk])
            cp("dve", loraT[:], bkb[0:64, 0:256], [Bbk], [BloraT])
            def prep_half(hf):
                cs = slice(hf * 512, (hf + 1) * 512)

                def R(bf):
                    return (bf, hf)
                w0_ap, Bw0 = prm("w0", hf * 512, 512)
                bk, Bbk = nb()
                mm(bk[:], loraT[:, 0:128], w2b[:, cs], True, True, [BloraT, Bw2b], [Bbk])
                tt("dve", tA[:, cs], bk[:], w0_ap, ALU.add, [Bbk, Bw0], [R(BtA)])
                yield
                act(tA[:, cs], tA[:, cs], AF.Exp, [R(BtA)], [R(BtA)], scale=-1.0)
                act(tA[:, cs], tA[:, cs], AF.Ln, [R(BtA)], [R(BtA)], bias=1.0)
                act(tA[:, cs], tA[:, cs], AF.Exp, [R(BtA)], [R(BtA)], scale=-1.0, bias=-0.5)
                yield
                a0_ap, Ba0 = prm("a0", hf * 512, 512)
                bk, Bbk = nb()
                mm(bk[:], loraT[:, 128:256], w2b[:, D + hf * 512:D + (hf + 1) * 512], True, True, [BloraT, Bw2b],
                   [Bbk])
                tt("dve", tB[:, cs], bk[:], a0_ap, ALU.add, [Bbk, Ba0], [R(BtB)])
                yield
                act(tB[:, cs], tB[:, cs], AF.Sigmoid, [R(BtB)], [R(BtB)])
                bk, Bbk = nb()
                mm(bk[:], tri_f, tA[:, cs], True, True, [Bcstf, R(BtA)], [Bbk])
                cp("act", tC[:, cs], bk[:], [Bbk], [R(BtC)])
                bk2, Bbk2 = nb()
                mm(bk2[:], ones_f, tA[:, cs], True, True, [Bcstf, R(BtA)], [Bbk2])
                tt("dve", tE[:, cs], bk2[:], tC[:, cs], ALU.subtract, [Bbk2, R(BtC)], [R(BtE)])
                yield
                bk, Bbk = nb()
                for i in range(8):
                    h = hf * 8 + i
                    mm(bk[0:64, 2 * i:2 * i + 2], tA[:, h * 64:(h + 1) * 64], ones_f[:, 0:2], True, True,
                       [R(BtA), Bcstf], [Bbk])
                act(pcfm[:, hf * 8:(hf + 1) * 8], strided_cols(bk[0:64, 0:1], 2, 8), AF.Exp, [Bbk], [R(Bpcfm)],
                    scale=-1.0)
                yield
                kk_ap, Bkk = prm("kk", hf * 512, 512)
                tt("dve", tF[:, cs], k_[:, cs], kk_ap, ALU.mult, [Bp_rw, Bkk], [R(BtF)])
                tt("pool", tG[:, cs], tF[:, cs], tF[:, cs], ALU.mult, [R(BtF)], [R(BtG)])
                yield
                smk = small[:, 24 + 8 * hf:32 + 8 * hf]
                Bsk = Bsm_kk2[hf]
                red(smk, v3(tG[:, cs], 8, 64), [R(BtG)], [Bsk])
                act(smk, smk, AF.Sqrt, [Bsk], [Bsk])
                ts("dve", smk, smk, 1e-12, None, ALU.max, None, [Bsk], [Bsk])
                recip(smk, smk, [Bsk], [Bsk])
                tt("dve", v3(tF[:, cs], 8, 64), v3(tF[:, cs], 8, 64), bc_last(smk, 64), ALU.mult,
                   [R(BtF), Bsk], [R(BtF)])
                yield
                ka_ap, Bka = prm("ka", hf * 512, 512)
                stt(tG[:, cs], tB[:, cs], -1.0, ka_ap, ALU.add, ALU.mult, [R(BtB), Bka], [R(BtG)])
                stt(tG[:, cs], tG[:, cs], 1.0, k_[:, cs], ALU.add, ALU.mult, [R(BtG), Bp_rw], [R(BtG)])
                tt("pool", tB[:, cs], tF[:, cs], tB[:, cs], ALU.mult, [R(BtF), R(BtB)], [R(BtB)])
                yield
                act(tH[:, cs], tC[:, cs], AF.Exp, [R(BtC)], [R(BtH)], scale=-1.0)
                tt("dve", tl["r_t"][:, cs], r_[:, cs], tH[:, cs], ALU.mult, [Bp_rw, R(BtH)], [R(Btl["r_t"])])
                yield
                tt("pool", tH[:, cs], tC[:, cs], tA[:, cs], ALU.subtract, [R(BtC), R(BtA), R(Btl["r_t"])], [R(BtH)])
                act(tH[:, cs], tH[:, cs], AF.Exp, [R(BtH)], [R(BtH)], scale=-1.0)
                stt(tl["a_t"][:, cs], tF[:, cs], -1.0, tH[:, cs], ALU.mult, ALU.mult, [R(BtF), R(BtH)],
                    [R(Btl["a_t"])])
                yield
                act(tH[:, cs], tC[:, cs], AF.Exp, [R(BtC), R(Btl["a_t"])], [R(BtH)])
                tt("dve", tl["b_t"][:, cs], tB[:, cs], tH[:, cs], ALU.mult, [R(BtB), R(BtH)], [R(Btl["b_t"])])
                tt("pool", tl["k_t"][:, cs], tG[:, cs], tH[:, cs], ALU.mult, [R(BtG), R(BtH)], [R(Btl["k_t"])])
                yield
                act(tE[:, cs], tE[:, cs], AF.Exp, [R(BtE)], [R(BtE)], scale=-1.0)
                tt("dve", tl["b_h"][:, cs], tB[:, cs], tE[:, cs], ALU.mult, [R(BtB), R(BtE)], [R(Btl["b_h"])])
                tt("pool", tl["k_h"][:, cs], tG[:, cs], tE[:, cs], ALU.mult, [R(BtG), R(BtE)], [R(Btl["k_h"])])
                cp("act", tl["v_b"][:, cs], v_[:, cs], [Bp_rw], [R(Btl["v_b"])])
                yield
                for ti, (nm_t, nm_f) in enumerate((("r_t", "rT"), ("a_t", "aT"), ("b_t", "bT"), ("k_t", "kT"))):
                    bk, Bbk = nb()
                    bkb = bk.bitcast(BF16)
                    for i in range(8):
                        h = hf * 8 + i
                        tr(bkb[0:64, i * 128:(i + 1) * 128], tl[nm_t][:, h * 64:(h + 1) * 64], [R(Btl[nm_t])], [Bbk])
                    cp("act" if (ti + hf) % 2 == 0 else "dve", fm[nm_f][:, hf * 1024:(hf + 1) * 1024],
                       bkb[0:64, 0:1024], [Bbk], [R(Bfm[nm_f])])
                    yield

            pg = [prep_half(0), prep_half(1)]
            while pg:
                for g_ in list(pg):
                    try:
                        next(g_)
                    except StopIteration:
                        pg.remove(g_)
            rT, aT, bT, kTr = fm["rT"], fm["aT"], fm["bT"], fm["kT"]
            BrT, BaT, BbT, BkTr = Bfm["rT"], Bfm["aT"], Bfm["bT"], Bfm["kT"]
            vb = tl["v_b"]
            Bvb = Btl["v_b"]

            def attn_gen():
                kTp, BkTp = kT[1 - par]
                vap, Bvap = vaug[1 - par]
                msk = maskA1 if b == 1 else maskA
                for hq in range(4):
                    ob, Bob = nb()
                    for hp2 in range(2):
                        hp = hq * 2 + hp2
                        g = hp // 4
                        bk, Bbk = nb()
                        for i in range(2):
                            h = hp * 2 + i
                            mm(bk[:, (2 * i) * 128:(2 * i + 1) * 128], kT_t[:, g * 128:(g + 1) * 128],
                               qT[:, h * 128:(h + 1) * 128], True, True, [BkT, BqT], [Bbk])
                            mm(bk[:, (2 * i + 1) * 128:(2 * i + 2) * 128], kTp[:, g * 128:(g + 1) * 128],
                               qT[:, h * 128:(h + 1) * 128], True, True, [BkTp, BqT], [Bbk])
                        pT_t, BpT = pT[hp % 2]
                        act(pT_t[:], bk[:], AF.Exp, [Bbk], [BpT], scale=0.125)
                        yield
                        tt("pool", pT_t[:], pT_t[:], msk, ALU.mult, [BpT, Bcstb], [BpT])
                        yield
                        for i in range(2):
                            h = hp * 2 + i
                            j = hp2 * 2 + i
                            mm(ob[:, j * 65:(j + 1) * 65], pT_t[:, (2 * i) * 128:(2 * i + 1) * 128], va_t[:, g, :],
                               True, False, [BpT, Bva], [Bob])
                            mm(ob[:, j * 65:(j + 1) * 65], pT_t[:, (2 * i + 1) * 128:(2 * i + 2) * 128],
                               vap[:, g, :], False, True, [BpT, Bvap], [Bob])
                        yield
                    ob3 = ob[:, 0:260].rearrange("p (a b) -> p a b", a=4, b=65)
                    den = small[:, 8 + hq * 4: 12 + hq * 4]
                    tt("dve", den, strided_cols(ob[:, 64:65], 65, 4), esk[:, hq * 4:(hq + 1) * 4], ALU.add,
                       [Bob, Besk], [Bsm_att])
                    recip(den, den, [Bsm_att], [Bsm_att])
                    tt("dve", v3(att[:, hq * 256:(hq + 1) * 256], 4, 64), ob3[:, :, 0:64], bc_last(den, 64),
                       ALU.mult, [Bob, Bsm_att], [Batt], waw=False)
                    yield
                tt("pool", gatt[:], att[:, 0:1024], p_sb[:, 1280:2304], ALU.mult, [Batt], [Bgatt])
                yield

            def bonus_gen():
                tt("pool", tA[:], r_, tG[:], ALU.mult, [Bp_rw, BtG], [BtA])
                yield
                rk_ap, Brk = prm("rk")
                tt("pool", tA[:], tA[:], rk_ap, ALU.mult, [BtA, Brk], [BtA])
                yield
                red(small[:, 64:80], v3(tA[:], 16, 64), [BtA], [Bsm_bon])
                yield
                tt("pool", v3(v_, 16, 64), v3(v_, 16, 64), bc_last(small[:, 64:80], 64), ALU.mult,
                   [Bp_rw, Bsm_bon], [Bp_rw])
                yield

            def quad_gen(Q):
                M = mats[Q % NS]
                hs = [Q * 4 + i for i in range(4)]

                def hsl(h):
                    return slice(h * 128, (h + 1) * 128)

                def isl(i):
                    return slice(i * 128, (i + 1) * 128)
                specs = (("N0", bT, BbT, aT, BaT, SU4), ("L0", aT, BaT, bT, BbT, SL4),
                         ("AK", kTr, BkTr, aT, BaT, SU4), ("RB", bT, BbT, rT, BrT, IU4),
                         ("RK", kTr, BkTr, rT, BrT, IU4))
                for (nm, lt, Blt, rh, Brh, msk_) in specs:
                    if b == 0 and nm in ("RB", "RK"):
                        continue
                    bk, Bbk = nb()
                    for i, h in enumerate(hs):
                        mm(bk[:, isl(i)], lt[:, hsl(h)], rh[:, hsl(h)], True, True, [Blt, Brh], [Bbk])
                    if nm in ("RB", "RK"):
                        cp("act", M[nm][0][:], bk[:], [Bbk], [M[nm][1]])
                        tt("pool", M[nm][0][:], M[nm][0][:], msk_, ALU.mult, [M[nm][1], Bcstb], [M[nm][1]])
                    else:
                        tt("dve", M[nm][0][:], bk[:], msk_, ALU.mult, [Bbk, Bcstb], [M[nm][1]])
                yield
                Xf, BXf = M["Xf"]
                Xb, BXb = M["Xb"]
                bk, Bbk = nb()
                for i, h in enumerate(hs):
                    osl = slice(i * 64, (i + 1) * 64)
                    mm(bk[:, osl], aT[:, hsl(h)], Hb[:, h * 64:(h + 1) * 64], True, True, [BaT, BHb[Q]], [Bbk])
                for i, h in enumerate(hs):
                    osl2 = slice(256 + i * 64, 256 + (i + 1) * 64)
                    mm(bk[:, osl2], M["AK"][0][:, isl(i)], vb[:, h * 64:(h + 1) * 64], True, True,
                       [M["AK"][1], Bvb], [Bbk])
                cp("act", Xf[:], bk[:, 0:256], [Bbk], [BXf])
                tt("dve", Xf[:], bk[:, 256:512], Xf[:], ALU.add, [Bbk, BXf], [BXf])
                cp("dve", Xb[:], Xf[:], [BXf], [BXb])
                yield
                for lev in range(7):
                    Nk, BNk = M["N%d" % (lev % 2)]
                    Lk, BLk = M["L%d" % (lev % 2)]
                    Nn, BNn = M["N%d" % ((lev + 1) % 2)]
                    Ln, BLn = M["L%d" % ((lev + 1) % 2)]
                    bk, Bbk = nb()
                    for i in range(4):
                        osl = slice(i * 64, (i + 1) * 64)
                        mm(bk[:, osl], Nk[:, isl(i)], Xb[:, osl], True, True, [BNk, BXb], [Bbk])
                    if lev < 6:
                        bk2, Bbk2 = nb()
                        for i in range(4):
                            mm(bk2[:, isl(i)], Lk[:, isl(i)], Nk[:, isl(i)], True, True, [BLk, BNk], [Bbk2])
                        if lev < 5:
                            bk3, Bbk3 = nb()
                            for i in range(4):
                                mm(bk3[:, isl(i)], Nk[:, isl(i)], Lk[:, isl(i)], True, True, [BLk, BNk], [Bbk3])
                    yield
                    tt("dve", Xf[:], bk[:, 0:256], Xf[:], ALU.add, [Bbk, BXf], [BXf])
                    cp("dve", Xb[:], Xf[:], [BXf], [BXb])
                    if lev < 6:
                        cp("act", Nn[:], bk2[:], [Bbk2], [BNn])
                        if lev < 5:
                            cp("act", Ln[:], bk3[:], [Bbk3], [BLn])
                    yield
                yield
                if b >= 1:
                    bk, Bbk = nb()
                    for i, h in enumerate(hs):
                        osl = slice(i * 64, (i + 1) * 64)
                        mm(bk[:, osl], rT[:, hsl(h)], Hb[:, h * 64:(h + 1) * 64], True, True, [BrT, BHb[Q]], [Bbk])
                    for i, h in enumerate(hs):
                        osl = slice(i * 64, (i + 1) * 64)
                        osl2 = slice(256 + i * 64, 256 + (i + 1) * 64)
                        mm(bk[:, osl2], M["RB"][0][:, isl(i)], Xb[:, osl], True, False, [M["RB"][1], BXb], [Bbk])
                        mm(bk[:, osl2], M["RK"][0][:, isl(i)], vb[:, h * 64:(h + 1) * 64], False, True,
                           [M["RK"][1], Bvb], [Bbk])
                    cp("act", y_sb[:, Q * 256:(Q + 1) * 256], bk[:, 0:256], [Bbk], [By])
                    tt("dve", y_sb[:, Q * 256:(Q + 1) * 256], bk[:, 256:512], y_sb[:, Q * 256:(Q + 1) * 256], ALU.add,
                       [Bbk, By], [By])
                yield
                bk, Bbk = nb()
                for i, h in enumerate(hs):
                    osl = slice(i * 64, (i + 1) * 64)
                    mm(bk[0:64, osl], tl["b_h"][:, h * 64:(h + 1) * 64], Xb[:, osl], True, False,
                       [Btl["b_h"], BXb], [Bbk])
                    mm(bk[0:64, osl], tl["k_h"][:, h * 64:(h + 1) * 64], vb[:, h * 64:(h + 1) * 64], False, True,
                       [Btl["k_h"], Bvb], [Bbk])
                hq_sl = slice(Q * 256, (Q + 1) * 256)
                tt("pool", v3(tmpH[:], 4, 64), v3(Hf[:, hq_sl], 4, 64), bc_last(pcfm[:, Q * 4:(Q + 1) * 4], 64),
                   ALU.mult, [BHf[Q], Bpcfm], [BtmpH])
                tt("dve", Hf[:, hq_sl], tmpH[:], bk[0:64, 0:256], ALU.add, [BtmpH, Bbk], [BHf[Q]])
                cp("act", Hb[:, hq_sl], Hf[:, hq_sl], [BHf[Q]], [BHb[Q]])

            extra = [attn_gen(), bonus_gen()] if b >= 1 else []
            for pair in range(2):
                gens = [quad_gen(pair * 2), quad_gen(pair * 2 + 1)] + extra
                nq = 2
                while nq > 0:
                    for g in list(gens):
                        try:
                            next(g)
                        except StopIteration:
                            gens.remove(g)
                            if g in extra:
                                extra.remove(g)
                            else:
                                nq -= 1
            for g in extra:
                for _ in g:
                    pass

        def back_gen(b):
            par = b % 2
            x_t, Bx = xs[par]
            uT, BuT = uTs2[par]
            v_ = p_sb[:, RW0 + 2048:RW0 + 3072]

            def v3(ap, a, bdim):
                return ap.rearrange("p (a b) -> p a b", a=a, b=bdim)
            y3 = v3(y_sb[:], 16, 64)
            red(small[:, 40:56], y3, [By], [Bsm_gn])
            ts("dve", small[:, 40:56], small[:, 40:56], 1.0 / 64, None, ALU.mult, None, [Bsm_gn], [Bsm_gn])
            tt("dve", y3, y3, bc_last(small[:, 40:56], 64), ALU.subtract, [By, Bsm_gn], [By])
            yield "g"
            tt("pool", tA[:], y_sb[:], y_sb[:], ALU.mult, [By], [BtA])
            red(small[:, 40:56], v3(tA[:], 16, 64), [BtA], [Bsm_gn])
            ts("dve", small[:, 40:56], small[:, 40:56], 1.0 / 64, GN_EPS, ALU.mult, ALU.add, [Bsm_gn], [Bsm_gn])
            act(small[:, 40:56], small[:, 40:56], AF.Sqrt, [Bsm_gn], [Bsm_gn])
            recip(small[:, 40:56], small[:, 40:56], [Bsm_gn], [Bsm_gn])
            yield "g"
            tt("dve", y3, y3, bc_last(small[:, 40:56], 64), ALU.mult, [By, Bsm_gn], [By])
            lnw_ap, Blnw = prm("lnw")
            tt("pool", y_sb[:], y_sb[:], lnw_ap, ALU.mult, [By, Blnw], [By])
            yield "g"
            lnb_ap, Blnb = prm("lnb")
            tt("dve", y_sb[:], y_sb[:], lnb_ap, ALU.add, [By, Blnb], [By])
            tt("pool", y_sb[:], y_sb[:], v_, ALU.add, [By, Bp_rw], [By])
            tt("dve", grw[:], y_sb[:], p_sb[:, GR0:GR0 + 1024], ALU.mult, [By, Bp_gr], [Bgrw])
            yield "gn"
            for (src, Bsrc, dst, Bdst) in ((gatt, Bgatt, gattT, BgattT), (grw, Bgrw, grwT, BgrwT)):
                bk, Bbk = nb()
                bkb = bk.bitcast(BF16)
                for c in range(8):
                    tr(bkb[:, c * 128:(c + 1) * 128], src[:, c * 128:(c + 1) * 128], [Bsrc], [Bbk])
                cp("act", dst[:], bkb[:, 0:1024], [Bbk], [Bdst])
                yield
            for br in range(2):
                for j in range(2):
                    wb, Bwb = load_w(win_bf, BWIN, NTM + br * 1024 + j * 512, 512)
                    bk, Bbk = nb()
                    for c in range(8):
                        mm(bk[:], uT[:, c * 128:(c + 1) * 128], wb[:, c, :], c == 0, c == 7, [Bwb, BuT], [Bbk])
                    act(sig[:, j * 512:(j + 1) * 512], bk[:], AF.Sigmoid, [Bbk], [Bsig], waw=(j == 0))
                    yield
                srcT, BsrcT = (gattT, BgattT) if br == 0 else (grwT, BgrwT)
                wsrc_, Bwsrc_ = (wba_bf, BWBA) if br == 0 else (wbr_bf, BWBR)
                dst, Bdst = (t1, Bt1) if br == 0 else (tA, BtA)
                for j in range(2):
                    wb, Bwb = load_w(wsrc_, Bwsrc_, j * 512, 512)
                    bk, Bbk = nb()
                    for c in range(8):
                        mm(bk[:], srcT[:, c * 128:(c + 1) * 128], wb[:, c, :], c == 0, c == 7, [Bwb, BsrcT], [Bbk])
                    tt("dve", dst[:, j * 512:(j + 1) * 512], bk[:], sig[:, j * 512:(j + 1) * 512], ALU.mult,
                       [Bbk, Bsig], [Bdst], waw=(j == 0))
                    yield
            tt("pool", gatt[:], t1[:], tA[:], ALU.add, [Bt1, BtA], [Bgatt])
            bk, Bbk = nb()
            bkb = bk.bitcast(BF16)
            for c in range(8):
                tr(bkb[:, c * 128:(c + 1) * 128], gatt[:, c * 128:(c + 1) * 128], [Bgatt], [Bbk])
            cp("act", mrgT[:], bkb[:, 0:1024], [Bbk], [BmrgT])
            yield
            for j in range(2):
                wb, Bwb = load_w(wout_bf, BWOUT, j * 512, 512)
                bk, Bbk = nb()
                for c in range(8):
                    mm(bk[:], mrgT[:, c * 128:(c + 1) * 128], wb[:, c, :], c == 0, c == 7, [BmrgT, Bwb], [Bbk])
                tt("dve", tB[:, j * 512:(j + 1) * 512], bk[:], x_t[:, j * 512:(j + 1) * 512], ALU.add,
                   [Bbk, Bx], [BtB], waw=False)
                yield
            o_t, Bo = osb[0]
            fnw_ap, Bfnw = prm("fnw")
            rmsnorm_scale(tB[:], BtB, fnw_ap, Bfnw, o_t[:], Bo, so=80, Bs=Bsm_fn)
            P.dma(y[(b - 1) * 128:b * 128, :], o_t[:], Bo, reads=[Bo], writes=[BY])


        def drain(g):
            for _ in g:
                pass

        drain(head_gen(0))
        for b in range(NBLK):
            mid(b)
            bg = back_gen(b) if b >= 1 else None
            hg = head_gen(b + 1) if b + 1 < NBLK else None
            if bg is not None:
                hsteps = 0
                for v_ in bg:
                    if v_ == "gn":
                        break
                    if hg is not None and hsteps < 9:
                        for _ in range(3):
                            if hsteps < 9:
                                next(hg)
                                hsteps += 1
            gens = [g for g in (bg, hg) if g is not None]
            while gens:
                for g in list(gens):
                    try:
                        next(g)
                    except StopIteration:
                        gens.remove(g)

        P.wait_all("sp", [BY])
        P.E["sp"].ops.append(lambda e: e.nop())

        with nc.Block() as block:
            @block.sync
            def _(e):
                for f in P.E["sp"].ops:
                    f(e)

            @block.tensor
            def _(e):
                for f in P.E["pe"].ops:
                    f(e)

            @block.scalar
            def _(e):
                for f in P.E["act"].ops:
                    f(e)

            @block.vector
            def _(e):
                for f in P.E["dve"].ops:
                    f(e)

            @block.gpsimd
            def _(e):
                for f in P.E["pool"].ops:
                    f(e)
        build.stats = {k: len(v.ops) for k, v in P.E.items()}
        build.nsem = P.nsem
        build.sbuf_free = nc.sbuf_bytes_remaining() if callable(getattr(nc, "sbuf_bytes_remaining", None)) else getattr(nc, "sbuf_bytes_remaining", None)
    return nc


def make_consts():
    i = np.arange(128)
    ident = np.eye(128, dtype=np.float32)
    mc = (i[:, None] <= i[None, :]).astype(np.float32)
    mp = (i[:, None] > i[None, :]).astype(np.float32)
    mp1 = mp * (i[:, None] >= PAD).astype(np.float32)
    su = (i[:, None] < i[None, :]).astype(np.float32)
    sl = (i[:, None] > i[None, :]).astype(np.float32)
    maskA = np.concatenate([mc, mp, mc, mp], 1)
    maskA1 = np.concatenate([mc, mp1, mc, mp1], 1)
    SU4 = np.tile(su, (1, 4))
    IU4 = np.tile(mc, (1, 4))
    SL4 = np.tile(sl, (1, 4))
    tri = mc
    ones = np.ones((128, 128), np.float32)
    sh = (i[:, None] + 1 == i[None, :]).astype(np.float32)
    em = np.zeros((128, 128), np.float32)
    em[127, 0] = 1.0
    c = np.concatenate([ident, maskA, maskA1, SU4, IU4, SL4, tri, ones, sh, em], 1).astype(np.float32)
    assert c.shape == (128, NCONST)
    return np.ascontiguousarray(c)


def make_tabs(NBLK):
    half = 32
    inv = (1.0 / (np.float32(10000.0) ** (np.arange(half, dtype=np.float32) / np.float32(half)))).astype(np.float32)
    pos = (np.arange(NBLK * 128, dtype=np.float32) - np.float32(PAD)).astype(np.float32)
    ang = (pos[:, None] * inv[None, :]).astype(np.float32)
    return np.ascontiguousarray(np.concatenate([np.cos(ang), np.sin(ang)], 1).astype(np.float32))


_CACHE = {}


def run(inputs, NBLK, ncores):
    f = lambda a: np.ascontiguousarray(np.asarray(a, dtype=np.float32))
    x = f(inputs["x"])
    meta = f(inputs["meta_tokens"])
    nx = (NBLK - 1) * 128
    if NBLK not in _CACHE:
        _CACHE[NBLK] = build(NBLK)
    nc = _CACHE[NBLK]
    common = {
        "w_in": f(inputs["w_in"][0]), "w_ba": f(inputs["w_branch_att"][0]), "w_br": f(inputs["w_branch_rwkv"][0]),
        "w_out": f(inputs["w_out"][0]), "consts": make_consts(), "tabs": make_tabs(NBLK),
        "p_normw": f(inputs["norm_w"][0]).reshape(1, -1), "p_fnw": f(inputs["final_norm_w"]).reshape(1, -1),
        "p_mu": f(inputs["rk_mu"][0]).reshape(1, -1), "p_w0": f(inputs["rk_w0"][0]).reshape(1, -1),
        "p_a0": f(inputs["rk_a0"][0]).reshape(1, -1), "p_kk": f(inputs["rk_k_k"][0]).reshape(1, -1),
        "p_ka": f(inputs["rk_k_a"][0]).reshape(1, -1), "p_rk": f(inputs["rk_r_k"][0]).reshape(1, -1),
        "p_lnw": f(inputs["rk_ln_w"][0]).reshape(1, -1), "p_lnb": f(inputs["rk_ln_b"][0]).reshape(1, -1),
        "p_sinks": f(inputs["att_sinks"][0]).reshape(1, -1),
        "w2": f(inputs["rk_w2"][0]), "a2": f(inputs["rk_a2"][0]),
    }
    in_maps = []
    for cidx in range(ncores):
        h0 = np.zeros((NBLK * 128, D), np.float32)
        h0[PAD:128] = meta
        h0[128:] = x[cidx, :nx]
        m = dict(common)
        m["h0"] = h0
        in_maps.append(m)
    res = run_bass_kernel_spmd(nc, in_maps, core_ids=list(range(ncores)))
    return np.stack([np.asarray(r["y"], dtype=np.float32) for r in res.results], 0)


def kernel(**inputs):
    return run(inputs, 65, 8)
```

```python
import contextlib
import numpy as np
import concourse.bass as bass
import concourse.mybir as mybir
from concourse.bass_utils import run_bass_kernel_spmd

F32 = mybir.dt.float32
BF16 = mybir.dt.bfloat16
AF = mybir.ActivationFunctionType
ALU = mybir.AluOpType
AX = mybir.AxisListType

D = 1024
NIN = 8576
NTM = 6528
RW0 = 2304
RWN = 3200
GR0 = 5504
N_META = 16
PAD = 112
RMS_EPS = 1e-6
GN_EPS = 64e-5
NCONST = 128 + 5 * 512 + 512


class Buf:
    def __init__(self, name):
        self.name = name
        self.w = {}
        self.r = {}
        self.dsem = None
        self.dkey = None
        self.dcnt = 0
        self.subs = None

    def split(self, n):
        self.subs = [Buf("%s_s%d" % (self.name, i)) for i in range(n)]
        for sbuf in self.subs:
            sbuf.w = dict(self.w)
            sbuf.r = dict(self.r)
        return self


def expand(lst):
    out = []
    for b in lst:
        if isinstance(b, tuple):
            out.append(b[0].subs[b[1]] if b[0].subs else b[0])
        elif b.subs:
            out.extend(b.subs)
        else:
            out.append(b)
    return out


class Eng:
    def __init__(self, key, getter):
        self.key = key
        self.getter = getter
        self.cnt = 0
        self.seen = {}
        self.ops = []
        self.sem = None


class Prog:
    def __init__(self, nc, stack):
        self.nc = nc
        self.stack = stack
        self.sems = {}
        self.E = {}
        for key in ("pe", "act", "dve", "pool", "sp"):
            e = Eng(key, None)
            e.sem = stack.enter_context(nc.semaphore("s_" + key))
            self.sems[key] = e.sem
            self.E[key] = e
        self.nsem = 5

    def _deps(self, E, reads, writes, acc, waw):
        deps = {}

        def add(k, v):
            if deps.get(k, 0) < v:
                deps[k] = v
        for b in reads:
            for k, v in b.w.items():
                add(k, v)
            if getattr(b, "is_bank", False):
                for k, v in b.r.items():
                    if k != E.key:
                        add(k, v)
        for b in writes:
            if waw:
                for k, v in b.w.items():
                    if acc and k == E.key:
                        continue
                    add(k, v)
            for k, v in b.r.items():
                add(k, v)
        for k, v in deps.items():
            if E.seen.get(k, 0) < v:
                E.seen[k] = v
                sem = self.sems[k]
                E.ops.append(lambda e, sem=sem, v=v: e.wait_ge(sem, v))

    def op(self, ek, fn, reads=(), writes=(), acc=False, waw=True):
        E = self.E[ek]
        reads = expand(reads)
        writes = expand(writes)
        if ek != "pe":
            for b in reads:
                if getattr(b, "held", False):
                    b.held = False
        self._deps(E, reads, writes, acc, waw)
        E.cnt += 1
        cnt = E.cnt
        sem = E.sem
        E.ops.append(lambda e, fn=fn, sem=sem: fn(e).then_inc(sem, 1))
        for b in reads:
            b.r[E.key] = cnt
        for b in writes:
            b.w[E.key] = cnt

    def dma(self, out_ap, in_ap, sembuf, reads=(), writes=(), qk="sp"):
        Q = self.E[qk]
        reads = expand(reads)
        writes = expand(writes)
        self._deps(Q, reads, writes, False, True)
        sb = sembuf
        if sb.dsem is None:
            sb.dkey = "d_" + sb.name
            sb.dsem = self.stack.enter_context(self.nc.semaphore(sb.dkey))
            self.sems[sb.dkey] = sb.dsem
            self.nsem += 1
        if sb.dcnt > 0 and Q.seen.get(sb.dkey, 0) < sb.dcnt:
            Q.seen[sb.dkey] = sb.dcnt
            Q.ops.append(lambda e, sem=sb.dsem, v=sb.dcnt: e.wait_ge(sem, v))
        sb.dcnt += 16
        val = sb.dcnt
        dsem = sb.dsem
        Q.ops.append(lambda e, o=out_ap, i=in_ap, dsem=dsem: e.dma_start(out=o, in_=i).then_inc(dsem, 16))
        for b in reads:
            b.r[sb.dkey] = val
        for b in writes:
            b.w[sb.dkey] = val

    def wait_all(self, ek, bufs):
        E = self.E[ek]
        for b in expand(bufs):
            for k, v in list(b.w.items()) + list(b.r.items()):
                if E.seen.get(k, 0) < v:
                    E.seen[k] = v
                    sem = self.sems[k]
                    E.ops.append(lambda e, sem=sem, v=v: e.wait_ge(sem, v))


def bc_last(ap, n):
    return bass.AP(ap.tensor, ap.offset, [list(x) for x in ap.ap] + [[0, n]])


def bc_mid(ap, n):
    a = [list(x) for x in ap.ap]
    return bass.AP(ap.tensor, ap.offset, [a[0], [0, n]] + a[1:])


def build(NBLK):
    nc = bass.Bass("TRN2", target_bir_lowering=False)
    NOUT = NBLK - 1

    def din(name, shape):
        return nc.dram_tensor(name, list(shape), F32, kind="ExternalInput").ap()
    h0 = din("h0", [NBLK * 128, D])
    w_in = din("w_in", [D, NIN])
    w_ba = din("w_ba", [D, D])
    w_br = din("w_br", [D, D])
    w_out = din("w_out", [D, D])
    consts = din("consts", [128, NCONST])
    tabs = din("tabs", [NBLK * 128, 64])
    pvec = {}
    for nm, n in (("normw", D), ("fnw", D), ("mu", RWN), ("w0", D), ("a0", D), ("kk", D), ("ka", D),
                  ("rk", D), ("lnw", D), ("lnb", D), ("sinks", 16)):
        pvec[nm] = din("p_" + nm, [1, n])
    w2d = din("w2", [64, D])
    a2d = din("a2", [64, D])
    y = nc.dram_tensor("y", [NOUT * 128, D], F32, kind="ExternalOutput").ap()
    win_bf = nc.dram_tensor("win_bf", [8, 128, NIN], BF16, kind="Internal").ap()
    wba_bf = nc.dram_tensor("wba_bf", [8, 128, D], BF16, kind="Internal").ap()
    wbr_bf = nc.dram_tensor("wbr_bf", [8, 128, D], BF16, kind="Internal").ap()
    wout_bf = nc.dram_tensor("wout_bf", [8, 128, D], BF16, kind="Internal").ap()

    with contextlib.ExitStack() as stack:
        P = Prog(nc, stack)
        bufs = {}

        def sb(name, shape, dt=F32):
            t = stack.enter_context(nc.sbuf_tensor(name, list(shape), dt))
            b = Buf(name)
            bufs[name] = b
            return t, b

        def B(name):
            b = Buf(name)
            return b

        cstb, Bcstb = sb("cstb", [128, NCONST], BF16)
        cstf, Bcstf = sb("cstf", [128, 512])
        pslot = [sb("pslot%d" % i, [128, D]) for i in range(3)]
        esk, Besk = sb("esk", [128, 16])
        w2b, Bw2b = sb("w2b", [64, 2 * D], BF16)
        xs = [sb("xs%d" % i, [128, D]) for i in range(2)]
        tab = [sb("tab%d" % i, [128, 64]) for i in range(2)]
        NW = 4
        wbuf = [sb("wbuf%d" % i, [128, 8, 512], BF16) for i in range(NW)]
        u_bf, Bu = sb("u_bf", [128, D], BF16)
        uTs2 = [sb("uT%d" % i, [128, D], BF16) for i in range(2)]
        uTs_t, BuTs = sb("uTs", [128, D], BF16)
        ulast, Bulast = sb("ulast", [128, 8], BF16)
        junkH, BjunkH = sb("junkH", [128, D])
        shtmp = [sb("shtmp%d" % i, [128, 512]) for i in range(2)]
        Bsm_fn = B("sm_fn")
        p_sb, Bp_all = sb("p_sb", [128, NTM])
        Bp_att, Bp_rw, Bp_gr = B("p_att"), B("p_rw"), B("p_gr")
        small, Bsmall = sb("small", [128, 96])
        Bsm_att, Bsm_kk, Bsm_gn, Bsm_bon = B("sm_att"), B("sm_kk"), B("sm_gn"), B("sm_bon")
        tA, BtA = sb("tA", [128, D])
        tB, BtB = sb("tB", [128, D])
        tC, BtC = sb("tC", [128, D])
        tD, BtD = sb("tD", [128, D])
        tE, BtE = sb("tE", [128, D])
        tF, BtF = sb("tF", [128, D])
        tG, BtG = sb("tG", [128, D])
        tH, BtH = sb("tH", [128, D])
        q_rot, Bq_rot = sb("q_rot", [128, D], BF16)
        k_rot, Bk_rot = sb("k_rot", [128, 128], BF16)
        qT, BqT = sb("qT", [64, 16 * 128], BF16)
        kT = [sb("kT%d" % i, [64, 256], BF16) for i in range(2)]
        vaug = [sb("vaug%d" % i, [128, 2, 65], BF16) for i in range(2)]
        pT = [sb("pT%d" % i, [128, 512], BF16) for i in range(2)]
        y_sb, By = sb("y_sb", [128, D])
        att, Batt = p_sb, Bp_att
        gatt, Bgatt = sb("gatt", [128, D], BF16)
        grw, Bgrw = sb("grw", [128, D], BF16)
        gattT, BgattT = sb("gattT", [128, D], BF16)
        grwT, BgrwT = sb("grwT", [128, D], BF16)
        mrgT, BmrgT = sb("mrgT", [128, D], BF16)
        lora_bf, Blora = sb("lora_bf", [128, 128], BF16)
        loraT, BloraT = sb("loraT", [64, 256], BF16)
        sig, Bsig = sb("sig", [128, D])
        t1, Bt1 = sb("t1", [128, D])
        junk, Bjunk = t1, Bt1
        tl = {"r_t": q_rot, "a_t": gattT, "b_t": grwT, "k_t": mrgT}
        Btl = {"r_t": Bq_rot, "a_t": BgattT, "b_t": BgrwT, "k_t": BmrgT}
        for nm in ("b_h", "k_h", "v_b"):
            tl[nm], Btl[nm] = sb("tl_" + nm, [128, D], BF16)
        fm = {"rT": sig.bitcast(BF16)[0:64, :], "aT": t1.bitcast(BF16)[0:64, :],
              "bT": tD.bitcast(BF16)[0:64, :], "kT": tH.bitcast(BF16)[0:64, :]}
        Bfm = {"rT": Bsig, "aT": Bt1, "bT": BtD, "kT": BtH}
        NS = 2
        mats = []
        for s_ in range(NS):
            d_ = {}
            for nm in ("N0", "N1", "L0", "L1", "AK", "RB", "RK"):
                d_[nm] = sb("m%d_%s" % (s_, nm), [128, 512], BF16)
            d_["Xf"] = sb("m%d_Xf" % s_, [128, 256])
            d_["Xb"] = sb("m%d_Xb" % s_, [128, 256], BF16)
            mats.append(d_)
        Hf, _ = sb("Hf", [64, D])
        Hb, _ = sb("Hb", [64, D], BF16)
        BHf = [B("Hf%d" % i) for i in range(4)]
        BHb = [B("Hb%d" % i) for i in range(4)]
        tmpH, BtmpH = sb("tmpH", [64, 256])
        pcfm, Bpcfm = sb("pcfm", [64, 16])
        osb = [sb("osb%d" % i, [128, D]) for i in range(1)]
        carry = nc.dram_tensor("carry", [2, RWN], F32, kind="Internal").ap()
        BCARRY = [B("carry0"), B("carry1")]
        Bps0 = B("ps_row0")
        cst = p_sb[:, 0:NCONST]
        Bcst = Bp_all
        w2f = p_sb[0:64, NCONST:NCONST + 2 * D]
        Bw2f = B("w2f")
        stg_f = [(tB, B("stgf0")), (tC, B("stgf1"))]
        sigb = sig.bitcast(BF16)
        stg_b = [(sigb[:, 0:1024], B("stgb0")), (tA.bitcast(BF16)[:, 0:1024], B("stgb1"))]

        for bb_ in (BtA, BtB, BtC, BtD, BtE, BtF, BtG, BtH, Bq_rot, BgattT, BgrwT, BmrgT, Btl["b_h"], Btl["k_h"],
                    Btl["v_b"], Bsig, Bt1, Bpcfm):
            bb_.split(2)
        Bsm_kk2 = [B("sm_kk0"), B("sm_kk1")]

        banks = []
        for i in range(8):
            t = stack.enter_context(nc.psum_tensor("bank%d" % i, [128, 512], F32))
            bb_ = B("bank%d" % i)
            bb_.is_bank = True
            banks.append((t, bb_))
        bank_i = [0]

        def nb():
            for _ in range(8):
                t = banks[bank_i[0] % 8]
                bank_i[0] += 1
                if not getattr(t[1], "held", False):
                    t[1].held = True
                    return t
            raise RuntimeError("no free PSUM bank")

        BWIN, BWBA, BWBR, BWOUT = B("win"), B("wba"), B("wbr"), B("wout")
        BY = B("yout")

        ident = cstb[:, 0:128]
        maskA = cstb[:, 128:640]
        maskA1 = cstb[:, 640:1152]
        SU4 = cstb[:, 1152:1664]
        IU4 = cstb[:, 1664:2176]
        SL4 = cstb[:, 2176:2688]
        tri_f = cstf[:, 0:128]
        ones_f = cstf[:, 128:256]
        sh_f = cstf[:, 256:384]
        e_f = cstf[:, 384:512]

        def mm(out, lhsT, rhs, start, stop, reads, writes):
            P.op("pe", lambda e: e.matmul(out, lhsT, rhs, start=start, stop=stop), reads=reads, writes=writes,
                 acc=True)

        def tr(out, in_, reads, writes, np_=128):
            P.op("pe", lambda e: e.transpose(out, in_, ident[0:np_, 0:np_]), reads=list(reads) + [Bcstb],
                 writes=writes, acc=True)

        def act(out, in_, func, reads, writes, bias=None, scale=None, waw=True):
            kw = {}
            if bias is not None:
                kw["bias"] = bias
            if scale is not None:
                kw["scale"] = scale
            P.op("act", lambda e: e.activation(out, in_, func, **kw), reads=reads, writes=writes, waw=waw)

        def tt(ek, out, in0, in1, op, reads, writes, waw=True):
            P.op(ek, lambda e: e.tensor_tensor(out, in0, in1, op), reads=reads, writes=writes, waw=waw)

        def ts(ek, out, in0, s1, s2, op0, op1, reads, writes):
            if op1 is None:
                P.op(ek, lambda e: e.tensor_scalar(out, in0, s1, None, op0), reads=reads, writes=writes)
            else:
                P.op(ek, lambda e: e.tensor_scalar(out, in0, s1, s2, op0, op1), reads=reads, writes=writes)

        def stt(out, in0, scalar, in1, op0, op1, reads, writes):
            P.op("dve", lambda e: e.scalar_tensor_tensor(out, in0, scalar, in1, op0, op1), reads=reads,
                 writes=writes)

        def red(out, in_, reads, writes):
            P.op("dve", lambda e: e.tensor_reduce(out, in_, AX.X, ALU.add), reads=reads, writes=writes)

        def recip(out, in_, reads, writes):
            P.op("dve", lambda e: e.reciprocal(out, in_), reads=reads, writes=writes)

        def cp(ek, out, in_, reads, writes, waw=True):
            if ek == "act":
                act(out, in_, AF.Copy, reads, writes, waw=waw)
            else:
                P.op(ek, lambda e: e.tensor_copy(out, in_), reads=reads, writes=writes, waw=waw)

        def memset(ek, ap, val, writes):
            P.op(ek, lambda e: e.memset(ap, val), writes=writes)

        pcnt = [0]

        def prm(nm, c0=0, n=D):
            t, Bt = pslot[pcnt[0] % 3]
            pcnt[0] += 1
            src = bass.AP(pvec[nm].tensor, c0, [[0, 128], [1, n]])
            P.dma(t[:, 0:n], src, Bt, writes=[Bt])
            return t[:, 0:n], Bt

        def strided_cols(ap1, step, n):
            a = [list(x) for x in ap1.ap]
            return bass.AP(ap1.tensor, ap1.offset, [a[0], [step, n]])

        def merge(dst, src):
            for d1 in expand([dst]):
                for dd, sd in ((d1.r, src.w), (d1.r, src.r)):
                    for k, v in sd.items():
                        if dd.get(k, 0) < v:
                            dd[k] = v

        P.dma(cst, consts[:, :], Bcst, writes=[Bcst])
        cp("dve", cstb[:], cst, [Bcst], [Bcstb])
        cp("pool", cstf[:], p_sb[:, 2688:3200], [Bcst], [Bcstf])
        P.dma(esk[:], bass.AP(pvec["sinks"].tensor, 0, [[0, 128], [1, 16]]), Besk, writes=[Besk])
        act(esk[:], esk[:], AF.Exp, [Besk], [Besk])
        P.dma(w2f[:, 0:D], w2d[:, :], Bw2f, writes=[Bw2f])
        P.dma(w2f[:, D:2 * D], a2d[:, :], B("w2f_b"), writes=[Bw2f])
        cp("dve", w2b[:], w2f, [Bw2f], [Bw2b])
        si = 0
        cast_engs = ["dve", "pool", "act"]
        for (wsrc, wdst, Bd, ncol) in ((w_in, win_bf, BWIN, NIN), (w_ba, wba_bf, BWBA, D), (w_br, wbr_bf, BWBR, D),
                                       (w_out, wout_bf, BWOUT, D)):
            pw = 536 if ncol == NIN else 1024
            for c in range(8):
                for c0 in range(0, ncol, pw):
                    sf, Bsf = stg_f[si % 2]
                    sbf, Bsb = stg_b[si % 2]
                    P.dma(sf[:, 0:pw], wsrc[c * 128:(c + 1) * 128, c0:c0 + pw], Bsf, writes=[Bsf])
                    cp(cast_engs[si % 3], sbf[:, 0:pw], sf[:, 0:pw], [Bsf], [Bsb])
                    P.dma(wdst[c, :, c0:c0 + pw], sbf[:, 0:pw], Bsb, reads=[Bsb], writes=[Bd])
                    si += 1
        merge(BtB, stg_f[0][1])
        merge(BtC, stg_f[1][1])
        merge(Bsig, stg_b[0][1])
        merge(BtA, stg_b[1][1])
        for bb in (Bp_att, Bp_rw, Bp_gr):
            merge(bb, Bcst)
            merge(bb, Bw2f)
        memset("pool", ulast[:], 0.0, [Bulast])
        memset("pool", Hf[:], 0.0, BHf)
        memset("pool", Hb[:], 0.0, BHb)
        for i in range(2):
            memset("pool", vaug[i][0][:], 1.0, [vaug[i][1]])

        def load_x(b):
            par = b % 2
            P.dma(xs[par][0][:], h0[b * 128:(b + 1) * 128, :], xs[par][1], writes=[xs[par][1]])
            P.dma(tab[par][0][:], tabs[b * 128:(b + 1) * 128, :], tab[par][1], writes=[tab[par][1]])

        load_x(0)
        if NBLK > 1:
            load_x(1)
        wcnt = [0]

        def load_w(src, Bsrc, c0, ncols):
            wb, Bwb = wbuf[wcnt[0] % NW]
            wcnt[0] += 1
            P.dma(wb[:, :, 0:ncols], src[:, :, c0:c0 + ncols].rearrange("c p n -> p c n"), Bwb, reads=[Bsrc],
                  writes=[Bwb])
            return wb, Bwb

        def rmsnorm_scale(src, Bsrc, wrep, Bwrep, out, Bout, jk=None, Bjk=None, so=0, Bs=None):
            jk = junk if jk is None else jk
            Bjk = Bjunk if Bjk is None else Bjk
            Bs = Bsmall if Bs is None else Bs
            act(jk[:], src, AF.Square, [Bsrc], [Bjk])
            red(small[:, so:so + 1], jk[:], [Bjk], [Bs])
            ts("dve", small[:, so + 1:so + 2], small[:, so:so + 1], 1.0 / D, RMS_EPS, ALU.mult, ALU.add, [Bs], [Bs])
            act(small[:, so + 2:so + 3], small[:, so + 1:so + 2], AF.Sqrt, [Bs], [Bs])
            recip(small[:, so + 3:so + 4], small[:, so + 2:so + 3], [Bs], [Bs])
            stt(out, src, small[:, so + 3:so + 4], wrep, ALU.mult, ALU.mult, [Bsrc, Bs, Bwrep], [Bout])

        def head_gen(b):
            par = b % 2
            x_t, Bx = xs[par]
            uT, BuT = uTs2[par]
            nw_ap, Bnw = prm("normw")
            rmsnorm_scale(x_t[:], Bx, nw_ap, Bnw, u_bf[:], Bu, jk=junkH, Bjk=BjunkH)
            bk, Bbk = nb()
            bkb = bk.bitcast(BF16)
            for c in range(8):
                tr(bkb[:, c * 128:(c + 1) * 128], u_bf[:, c * 128:(c + 1) * 128], [Bu], [Bbk])
            cp("act", uT[:], bkb[:, 0:1024], [Bbk], [BuT])
            uT3 = uT[:].rearrange("p (c t) -> p c t", c=8, t=128)
            uS3 = uTs_t[:].rearrange("p (c t) -> p c t", c=8, t=128)
            cp("pool", uS3[:, :, 1:128], uT3[:, :, 0:127], [BuT], [BuTs])
            cp("pool", uS3[:, :, 0:1], ulast[:].rearrange("p (c o) -> p c o", c=8, o=1), [Bulast], [BuTs], waw=False)
            cp("pool", ulast[:].rearrange("p (c o) -> p c o", c=8, o=1), uT3[:, :, 127:128], [BuT], [Bulast])
            yield
            for nt in range(13):
                n0 = nt * 512
                nsz = min(512, NTM - n0)
                wb, Bwb = load_w(win_bf, BWIN, n0, nsz)
                bk, Bbk = nb()
                for c in range(8):
                    mm(bk[:, 0:nsz], uT[:, c * 128:(c + 1) * 128], wb[:, c, 0:nsz], c == 0, c == 7,
                       [BuT, Bwb], [Bbk])
                r0, r1 = max(n0, RW0), min(n0 + nsz, GR0)
                if r1 > r0:
                    bk2, Bbk2 = nb()
                    for c in range(8):
                        mm(bk2[:, r0 - n0:r1 - n0], uTs_t[:, c * 128:(c + 1) * 128], wb[:, c, r0 - n0:r1 - n0],
                           c == 0, c == 7, [BuTs, Bwb], [Bbk2])
                segs = ((0, 1280, False, Bp_att), (1280, RW0, True, Bp_att), (RW0, GR0, False, Bp_rw),
                        (GR0, NTM, True, Bp_gr))
                has_silu = any(sg[2] and sg[0] < n0 + nsz and sg[1] > n0 for sg in segs)
                for (s0, s1, is_silu, Bseg) in segs:
                    a0, a1 = max(n0, s0), min(n0 + nsz, s1)
                    if a1 <= a0:
                        continue
                    if is_silu:
                        act(p_sb[:, a0:a1], bk[:, a0 - n0:a1 - n0], AF.Silu, [Bbk], [Bseg], waw=False)
                    else:
                        eng = "act" if (has_silu or nt % 2 == 0) else "dve"
                        cp(eng, p_sb[:, a0:a1], bk[:, a0 - n0:a1 - n0], [Bbk], [Bseg], waw=False)
                if r1 > r0:
                    n_ = r1 - r0
                    st_t, Bst = shtmp[nt % 2]
                    mu_ap, Bmu = prm("mu", r0 - RW0, n_)
                    tt("dve", st_t[:, 0:n_], bk2[:, r0 - n0:r1 - n0], p_sb[:, r0:r1], ALU.subtract,
                       [Bbk2, Bp_rw], [Bst])
                    tt("pool", st_t[:, 0:n_], st_t[:, 0:n_], mu_ap, ALU.mult, [Bst, Bmu], [Bst])
                    tt("dve" if nt % 2 == 0 else "pool", p_sb[:, r0:r1], st_t[:, 0:n_], p_sb[:, r0:r1], ALU.add,
                       [Bst, Bp_rw], [Bp_rw], waw=False)
                yield

        def mid(b):
            par = b % 2
            x_t, Bx = xs[par]
            tb_t, Btb = tab[par]
            uT, BuT = uTs2[par]
            if b >= 1 and b + 1 < NBLK:
                load_x(b + 1)

            def v3(ap, a, bdim):
                return ap.rearrange("p (a b) -> p a b", a=a, b=bdim)
            q3 = v3(p_sb[:, 0:1024], 16, 64)
            qr3 = v3(q_rot[:], 16, 64)
            cosb = bc_mid(tb_t[:, 0:32], 16)
            sinb = bc_mid(tb_t[:, 32:64], 16)
            A3 = v3(tA[:, 0:512], 16, 32)
            B3 = v3(tB[:, 0:512], 16, 32)
            C3 = v3(tA[:, 512:1024], 16, 32)
            D3 = v3(tB[:, 512:1024], 16, 32)
            tt("dve", A3, q3[:, :, 0:32], cosb, ALU.mult, [Bp_att, Btb], [BtA])
            tt("pool", B3, q3[:, :, 32:64], sinb, ALU.mult, [Bp_att, Btb], [BtB])
            tt("dve", qr3[:, :, 0:32], A3, B3, ALU.subtract, [BtA, BtB], [Bq_rot])
            tt("pool", C3, q3[:, :, 32:64], cosb, ALU.mult, [Bp_att, Btb], [BtA])
            tt("dve", D3, q3[:, :, 0:32], sinb, ALU.mult, [Bp_att, Btb], [BtB])
            tt("pool", qr3[:, :, 32:64], C3, D3, ALU.add, [BtA, BtB], [Bq_rot])
            k3 = v3(p_sb[:, 1024:1152], 2, 64)
            kr3 = v3(k_rot[:], 2, 64)
            cos2 = bc_mid(tb_t[:, 0:32], 2)
            sin2 = bc_mid(tb_t[:, 32:64], 2)
            E3 = v3(tC[:, 0:64], 2, 32)
            F3 = v3(tC[:, 64:128], 2, 32)
            G3 = v3(tC[:, 128:192], 2, 32)
            H3 = v3(tC[:, 192:256], 2, 32)
            tt("dve", E3, k3[:, :, 0:32], cos2, ALU.mult, [Bp_att, Btb], [BtC])
            tt("dve", F3, k3[:, :, 32:64], sin2, ALU.mult, [Bp_att, Btb], [BtC])
            tt("dve", kr3[:, :, 0:32], E3, F3, ALU.subtract, [BtC], [Bk_rot])
            tt("dve", G3, k3[:, :, 32:64], cos2, ALU.mult, [Bp_att, Btb], [BtC])
            tt("dve", H3, k3[:, :, 0:32], sin2, ALU.mult, [Bp_att, Btb], [BtC])
            tt("dve", kr3[:, :, 32:64], G3, H3, ALU.add, [BtC], [Bk_rot])
            va_t, Bva = vaug[par]
            cp("pool", va_t[:, :, 0:64], v3(p_sb[:, 1152:1280], 2, 64), [Bp_att], [Bva])
            kT_t, BkT = kT[par]
            bk, Bbk = nb()
            bkb = bk.bitcast(BF16)
            for g in range(2):
                tr(bkb[0:64, g * 128:(g + 1) * 128], k_rot[:, g * 64:(g + 1) * 64], [Bk_rot], [Bbk])
            cp("dve", kT_t[:], bkb[0:64, 0:256], [Bbk], [BkT])
            if b >= 1:
                for hh in range(2):
                    bk, Bbk = nb()
                    bkb = bk.bitcast(BF16)
                    for i in range(8):
                        h = hh * 8 + i
                        tr(bkb[0:64, i * 128:(i + 1) * 128], q_rot[:, h * 64:(h + 1) * 64], [Bq_rot], [Bbk])
                    cp("act" if hh == 0 else "dve", qT[:, hh * 1024:(hh + 1) * 1024], bkb[0:64, 0:1024], [Bbk],
                       [BqT], waw=False)

            r_ = p_sb[:, RW0:RW0 + 1024]
            k_ = p_sb[:, RW0 + 1024:RW0 + 2048]
            v_ = p_sb[:, RW0 + 2048:RW0 + 3072]
            wl_ = p_sb[:, RW0 + 3072:RW0 + 3136]
            al_ = p_sb[:, RW0 + 3136:RW0 + 3200]
            act(lora_bf[:, 0:64], wl_, AF.Tanh, [Bp_rw], [Blora])
            cp("dve", lora_bf[:, 64:128], al_, [Bp_rw], [Blora])
            bk, Bbk = nb()
            bkb = bk.bitcast(BF16)
            tr(bkb[0:64, 0:128], lora_bf[:, 0:64], [Blora], [Bbk])
            tr(bkb[0:64, 128:256], lora_bf[:, 64:128], [Blora], [Bbk])
            cp("dve", loraT[:], bkb[0:64, 0:256], [Bbk], [BloraT])
            def prep_half(hf):
                cs = slice(hf * 512, (hf + 1) * 512)

                def R(bf):
                    return (bf, hf)
                w0_ap, Bw0 = prm("w0", hf * 512, 512)
                bk, Bbk = nb()
                mm(bk[:], loraT[:, 0:128], w2b[:, cs], True, True, [BloraT, Bw2b], [Bbk])
                tt("dve", tA[:, cs], bk[:], w0_ap, ALU.add, [Bbk, Bw0], [R(BtA)])
                yield
                act(tA[:, cs], tA[:, cs], AF.Exp, [R(BtA)], [R(BtA)], scale=-1.0)
                act(tA[:, cs], tA[:, cs], AF.Ln, [R(BtA)], [R(BtA)], bias=1.0)
                act(tA[:, cs], tA[:, cs], AF.Exp, [R(BtA)], [R(BtA)], scale=-1.0, bias=-0.5)
                yield
                a0_ap, Ba0 = prm("a0", hf * 512, 512)
                bk, Bbk = nb()
                mm(bk[:], loraT[:, 128:256], w2b[:, D + hf * 512:D + (hf + 1) * 512], True, True, [BloraT, Bw2b],
                   [Bbk])
                tt("dve", tB[:, cs], bk[:], a0_ap, ALU.add, [Bbk, Ba0], [R(BtB)])
                yield
                act(tB[:, cs], tB[:, cs], AF.Sigmoid, [R(BtB)], [R(BtB)])
                bk, Bbk = nb()
                mm(bk[:], tri_f, tA[:, cs], True, True, [Bcstf, R(BtA)], [Bbk])
                cp("act", tC[:, cs], bk[:], [Bbk], [R(BtC)])
                bk2, Bbk2 = nb()
                mm(bk2[:], ones_f, tA[:, cs], True, True, [Bcstf, R(BtA)], [Bbk2])
                tt("dve", tE[:, cs], bk2[:], tC[:, cs], ALU.subtract, [Bbk2, R(BtC)], [R(BtE)])
                yield
                bk, Bbk = nb()
                for i in range(8):
                    h = hf * 8 + i
                    mm(bk[0:64, 2 * i:2 * i + 2], tA[:, h * 64:(h + 1) * 64], ones_f[:, 0:2], True, True,
                       [R(BtA), Bcstf], [Bbk])
                act(pcfm[:, hf * 8:(hf + 1) * 8], strided_cols(bk[0:64, 0:1], 2, 8), AF.Exp, [Bbk], [R(Bpcfm)],
                    scale=-1.0)
                yield
                kk_ap, Bkk = prm("kk", hf * 512, 512)
                tt("dve", tF[:, cs], k_[:, cs], kk_ap, ALU.mult, [Bp_rw, Bkk], [R(BtF)])
                tt("pool", tG[:, cs], tF[:, cs], tF[:, cs], ALU.mult, [R(BtF)], [R(BtG)])
                yield
                smk = small[:, 24 + 8 * hf:32 + 8 * hf]
                Bsk = Bsm_kk2[hf]
                red(smk, v3(tG[:, cs], 8, 64), [R(BtG)], [Bsk])
                act(smk, smk, AF.Sqrt, [Bsk], [Bsk])
                ts("dve", smk, smk, 1e-12, None, ALU.max, None, [Bsk], [Bsk])
                recip(smk, smk, [Bsk], [Bsk])
                tt("dve", v3(tF[:, cs], 8, 64), v3(tF[:, cs], 8, 64), bc_last(smk, 64), ALU.mult,
                   [R(BtF), Bsk], [R(BtF)])
                yield
                ka_ap, Bka = prm("ka", hf * 512, 512)
                stt(tG[:, cs], tB[:, cs], -1.0, ka_ap, ALU.add, ALU.mult, [R(BtB), Bka], [R(BtG)])
                stt(tG[:, cs], tG[:, cs], 1.0, k_[:, cs], ALU.add, ALU.mult, [R(BtG), Bp_rw], [R(BtG)])
                tt("pool", tB[:, cs], tF[:, cs], tB[:, cs], ALU.mult, [R(BtF), R(BtB)], [R(BtB)])
                yield
                act(tH[:, cs], tC[:, cs], AF.Exp, [R(BtC)], [R(BtH)], scale=-1.0)
                tt("dve", tl["r_t"][:, cs], r_[:, cs], tH[:, cs], ALU.mult, [Bp_rw, R(BtH)], [R(Btl["r_t"])])
                yield
                tt("pool", tH[:, cs], tC[:, cs], tA[:, cs], ALU.subtract, [R(BtC), R(BtA), R(Btl["r_t"])], [R(BtH)])
                act(tH[:, cs], tH[:, cs], AF.Exp, [R(BtH)], [R(BtH)], scale=-1.0)
                stt(tl["a_t"][:, cs], tF[:, cs], -1.0, tH[:, cs], ALU.mult, ALU.mult, [R(BtF), R(BtH)],
                    [R(Btl["a_t"])])
                yield
                act(tH[:, cs], tC[:, cs], AF.Exp, [R(BtC), R(Btl["a_t"])], [R(BtH)])
                tt("dve", tl["b_t"][:, cs], tB[:, cs], tH[:, cs], ALU.mult, [R(BtB), R(BtH)], [R(Btl["b_t"])])
                tt("pool", tl["k_t"][:, cs], tG[:, cs], tH[:, cs], ALU.mult, [R(BtG), R(BtH)], [R(Btl["k_t"])])
                yield
                act(tE[:, cs], tE[:, cs], AF.Exp, [R(BtE)], [R(BtE)], scale=-1.0)
                tt("dve", tl["b_h"][:, cs], tB[:, cs], tE[:, cs], ALU.mult, [R(BtB), R(BtE)], [R(Btl["b_h"])])
                tt("pool", tl["k_h"][:, cs], tG[:, cs], tE[:, cs], ALU.mult, [R(BtG), R(BtE)], [R(Btl["k_h"])])
                cp("act", tl["v_b"][:, cs], v_[:, cs], [Bp_rw], [R(Btl["v_b"])])
                yield
                for ti, (nm_t, nm_f) in enumerate((("r_t", "rT"), ("a_t", "aT"), ("b_t", "bT"), ("k_t", "kT"))):
                    bk, Bbk = nb()
                    bkb = bk.bitcast(BF16)
                    for i in range(8):
                        h = hf * 8 + i
                        tr(bkb[0:64, i * 128:(i + 1) * 128], tl[nm_t][:, h * 64:(h + 1) * 64], [R(Btl[nm_t])], [Bbk])
                    cp("act" if (ti + hf) % 2 == 0 else "dve", fm[nm_f][:, hf * 1024:(hf + 1) * 1024],
                       bkb[0:64, 0:1024], [Bbk], [R(Bfm[nm_f])])
                    yield

            pg = [prep_half(0), prep_half(1)]
            while pg:
                for g_ in list(pg):
                    try:
                        next(g_)
                    except StopIteration:
                        pg.remove(g_)
            rT, aT, bT, kTr = fm["rT"], fm["aT"], fm["bT"], fm["kT"]
            BrT, BaT, BbT, BkTr = Bfm["rT"], Bfm["aT"], Bfm["bT"], Bfm["kT"]
            vb = tl["v_b"]
            Bvb = Btl["v_b"]

            def attn_gen():
                kTp, BkTp = kT[1 - par]
                vap, Bvap = vaug[1 - par]
                msk = maskA1 if b == 1 else maskA
                for hq in range(4):
                    ob, Bob = nb()
                    for hp2 in range(2):
                        hp = hq * 2 + hp2
                        g = hp // 4
                        bk, Bbk = nb()
                        for i in range(2):
                            h = hp * 2 + i
                            mm(bk[:, (2 * i) * 128:(2 * i + 1) * 128], kT_t[:, g * 128:(g + 1) * 128],
                               qT[:, h * 128:(h + 1) * 128], True, True, [BkT, BqT], [Bbk])
                            mm(bk[:, (2 * i + 1) * 128:(2 * i + 2) * 128], kTp[:, g * 128:(g + 1) * 128],
                               qT[:, h * 128:(h + 1) * 128], True, True, [BkTp, BqT], [Bbk])
                        pT_t, BpT = pT[hp % 2]
                        act(pT_t[:], bk[:], AF.Exp, [Bbk], [BpT], scale=0.125)
                        yield
                        tt("pool", pT_t[:], pT_t[:], msk, ALU.mult, [BpT, Bcstb], [BpT])
                        yield
                        for i in range(2):
                            h = hp * 2 + i
                            j = hp2 * 2 + i
                            mm(ob[:, j * 65:(j + 1) * 65], pT_t[:, (2 * i) * 128:(2 * i + 1) * 128], va_t[:, g, :],
                               True, False, [BpT, Bva], [Bob])
                            mm(ob[:, j * 65:(j + 1) * 65], pT_t[:, (2 * i + 1) * 128:(2 * i + 2) * 128],
                               vap[:, g, :], False, True, [BpT, Bvap], [Bob])
                        yield
                    ob3 = ob[:, 0:260].rearrange("p (a b) -> p a b", a=4, b=65)
                    den = small[:, 8 + hq * 4: 12 + hq * 4]
                    tt("dve", den, strided_cols(ob[:, 64:65], 65, 4), esk[:, hq * 4:(hq + 1) * 4], ALU.add,
                       [Bob, Besk], [Bsm_att])
                    recip(den, den, [Bsm_att], [Bsm_att])
                    tt("dve", v3(att[:, hq * 256:(hq + 1) * 256], 4, 64), ob3[:, :, 0:64], bc_last(den, 64),
                       ALU.mult, [Bob, Bsm_att], [Batt], waw=False)
                    yield
                tt("pool", gatt[:], att[:, 0:1024], p_sb[:, 1280:2304], ALU.mult, [Batt], [Bgatt])
                yield

            def bonus_gen():
                tt("pool", tA[:], r_, tG[:], ALU.mult, [Bp_rw, BtG], [BtA])
                yield
                rk_ap, Brk = prm("rk")
                tt("pool", tA[:], tA[:], rk_ap, ALU.mult, [BtA, Brk], [BtA])
                yield
                red(small[:, 64:80], v3(tA[:], 16, 64), [BtA], [Bsm_bon])
                yield
                tt("pool", v3(v_, 16, 64), v3(v_, 16, 64), bc_last(small[:, 64:80], 64), ALU.mult,
                   [Bp_rw, Bsm_bon], [Bp_rw])
                yield

            def quad_gen(Q):
                M = mats[Q % NS]
                hs = [Q * 4 + i for i in range(4)]

                def hsl(h):
                    return slice(h * 128, (h + 1) * 128)

                def isl(i):
                    return slice(i * 128, (i + 1) * 128)
                specs = (("N0", bT, BbT, aT, BaT, SU4), ("L0", aT, BaT, bT, BbT, SL4),
                         ("AK", kTr, BkTr, aT, BaT, SU4), ("RB", bT, BbT, rT, BrT, IU4),
                         ("RK", kTr, BkTr, rT, BrT, IU4))
                for (nm, lt, Blt, rh, Brh, msk_) in specs:
                    if b == 0 and nm in ("RB", "RK"):
                        continue
                    bk, Bbk = nb()
                    for i, h in enumerate(hs):
                        mm(bk[:, isl(i)], lt[:, hsl(h)], rh[:, hsl(h)], True, True, [Blt, Brh], [Bbk])
                    if nm in ("RB", "RK"):
                        cp("act", M[nm][0][:], bk[:], [Bbk], [M[nm][1]])
                        tt("pool", M[nm][0][:], M[nm][0][:], msk_, ALU.mult, [M[nm][1], Bcstb], [M[nm][1]])
                    else:
                        tt("dve", M[nm][0][:], bk[:], msk_, ALU.mult, [Bbk, Bcstb], [M[nm][1]])
                yield
                Xf, BXf = M["Xf"]
                Xb, BXb = M["Xb"]
                bk, Bbk = nb()
                for i, h in enumerate(hs):
                    osl = slice(i * 64, (i + 1) * 64)
                    mm(bk[:, osl], aT[:, hsl(h)], Hb[:, h * 64:(h + 1) * 64], True, True, [BaT, BHb[Q]], [Bbk])
                for i, h in enumerate(hs):
                    osl2 = slice(256 + i * 64, 256 + (i + 1) * 64)
                    mm(bk[:, osl2], M["AK"][0][:, isl(i)], vb[:, h * 64:(h + 1) * 64], True, True,
                       [M["AK"][1], Bvb], [Bbk])
                cp("act", Xf[:], bk[:, 0:256], [Bbk], [BXf])
                tt("dve", Xf[:], bk[:, 256:512], Xf[:], ALU.add, [Bbk, BXf], [BXf])
                cp("dve", Xb[:], Xf[:], [BXf], [BXb])
                yield
                for lev in range(7):
                    Nk, BNk = M["N%d" % (lev % 2)]
                    Lk, BLk = M["L%d" % (lev % 2)]
                    Nn, BNn = M["N%d" % ((lev + 1) % 2)]
                    Ln, BLn = M["L%d" % ((lev + 1) % 2)]
                    bk, Bbk = nb()
                    for i in range(4):
                        osl = slice(i * 64, (i + 1) * 64)
                        mm(bk[:, osl], Nk[:, isl(i)], Xb[:, osl], True, True, [BNk, BXb], [Bbk])
                    if lev < 6:
                        bk2, Bbk2 = nb()
                        for i in range(4):
                            mm(bk2[:, isl(i)], Lk[:, isl(i)], Nk[:, isl(i)], True, True, [BLk, BNk], [Bbk2])
                        if lev < 5:
                            bk3, Bbk3 = nb()
                            for i in range(4):
                                mm(bk3[:, isl(i)], Nk[:, isl(i)], Lk[:, isl(i)], True, True, [BLk, BNk], [Bbk3])
                    yield
                    tt("dve", Xf[:], bk[:, 0:256], Xf[:], ALU.add, [Bbk, BXf], [BXf])
                    cp("dve", Xb[:], Xf[:], [BXf], [BXb])
                    if lev < 6:
                        cp("act", Nn[:], bk2[:], [Bbk2], [BNn])
                        if lev < 5:
                            cp("act", Ln[:], bk3[:], [Bbk3], [BLn])
                    yield
                yield
                if b >= 1:
                    bk, Bbk = nb()
                    for i, h in enumerate(hs):
                        osl = slice(i * 64, (i + 1) * 64)
                        mm(bk[:, osl], rT[:, hsl(h)], Hb[:, h * 64:(h + 1) * 64], True, True, [BrT, BHb[Q]], [Bbk])
                    for i, h in enumerate(hs):
                        osl = slice(i * 64, (i + 1) * 64)
                        osl2 = slice(256 + i * 64, 256 + (i + 1) * 64)
                        mm(bk[:, osl2], M["RB"][0][:, isl(i)], Xb[:, osl], True, False, [M["RB"][1], BXb], [Bbk])
                        mm(bk[:, osl2], M["RK"][0][:, isl(i)], vb[:, h * 64:(h + 1) * 64], False, True,
                           [M["RK"][1], Bvb], [Bbk])
                    cp("act", y_sb[:, Q * 256:(Q + 1) * 256], bk[:, 0:256], [Bbk], [By])
                    tt("dve", y_sb[:, Q * 256:(Q + 1) * 256], bk[:, 256:512], y_sb[:, Q * 256:(Q + 1) * 256], ALU.add,
                       [Bbk, By], [By])
                yield
                bk, Bbk = nb()
                for i, h in enumerate(hs):
                    osl = slice(i * 64, (i + 1) * 64)
                    mm(bk[0:64, osl], tl["b_h"][:, h * 64:(h + 1) * 64], Xb[:, osl], True, False,
                       [Btl["b_h"], BXb], [Bbk])
                    mm(bk[0:64, osl], tl["k_h"][:, h * 64:(h + 1) * 64], vb[:, h * 64:(h + 1) * 64], False, True,
                       [Btl["k_h"], Bvb], [Bbk])
                hq_sl = slice(Q * 256, (Q + 1) * 256)
                tt("pool", v3(tmpH[:], 4, 64), v3(Hf[:, hq_sl], 4, 64), bc_last(pcfm[:, Q * 4:(Q + 1) * 4], 64),
                   ALU.mult, [BHf[Q], Bpcfm], [BtmpH])
                tt("dve", Hf[:, hq_sl], tmpH[:], bk[0:64, 0:256], ALU.add, [BtmpH, Bbk], [BHf[Q]])
                cp("act", Hb[:, hq_sl], Hf[:, hq_sl], [BHf[Q]], [BHb[Q]])

            extra = [attn_gen(), bonus_gen()] if b >= 1 else []
            for pair in range(2):
                gens = [quad_gen(pair * 2), quad_gen(pair * 2 + 1)] + extra
                nq = 2
                while nq > 0:
                    for g in list(gens):
                        try:
                            next(g)
                        except StopIteration:
                            gens.remove(g)
                            if g in extra:
                                extra.remove(g)
                            else:
                                nq -= 1
            for g in extra:
                for _ in g:
                    pass

        def back_gen(b):
            par = b % 2
            x_t, Bx = xs[par]
            uT, BuT = uTs2[par]
            v_ = p_sb[:, RW0 + 2048:RW0 + 3072]

            def v3(ap, a, bdim):
                return ap.rearrange("p (a b) -> p a b", a=a, b=bdim)
            y3 = v3(y_sb[:], 16, 64)
            red(small[:, 40:56], y3, [By], [Bsm_gn])
            ts("dve", small[:, 40:56], small[:, 40:56], 1.0 / 64, None, ALU.mult, None, [Bsm_gn], [Bsm_gn])
            tt("dve", y3, y3, bc_last(small[:, 40:56], 64), ALU.subtract, [By, Bsm_gn], [By])
            yield "g"
            tt("pool", tA[:], y_sb[:], y_sb[:], ALU.mult, [By], [BtA])
            red(small[:, 40:56], v3(tA[:], 16, 64), [BtA], [Bsm_gn])
            ts("dve", small[:, 40:56], small[:, 40:56], 1.0 / 64, GN_EPS, ALU.mult, ALU.add, [Bsm_gn], [Bsm_gn])
            act(small[:, 40:56], small[:, 40:56], AF.Sqrt, [Bsm_gn], [Bsm_gn])
            recip(small[:, 40:56], small[:, 40:56], [Bsm_gn], [Bsm_gn])
            yield "g"
            tt("dve", y3, y3, bc_last(small[:, 40:56], 64), ALU.mult, [By, Bsm_gn], [By])
            lnw_ap, Blnw = prm("lnw")
            tt("pool", y_sb[:], y_sb[:], lnw_ap, ALU.mult, [By, Blnw], [By])
            yield "g"
            lnb_ap, Blnb = prm("lnb")
            tt("dve", y_sb[:], y_sb[:], lnb_ap, ALU.add, [By, Blnb], [By])
            tt("pool", y_sb[:], y_sb[:], v_, ALU.add, [By, Bp_rw], [By])
            tt("dve", grw[:], y_sb[:], p_sb[:, GR0:GR0 + 1024], ALU.mult, [By, Bp_gr], [Bgrw])
            yield "gn"
            for (src, Bsrc, dst, Bdst) in ((gatt, Bgatt, gattT, BgattT), (grw, Bgrw, grwT, BgrwT)):
                bk, Bbk = nb()
                bkb = bk.bitcast(BF16)
                for c in range(8):
                    tr(bkb[:, c * 128:(c + 1) * 128], src[:, c * 128:(c + 1) * 128], [Bsrc], [Bbk])
                cp("act", dst[:], bkb[:, 0:1024], [Bbk], [Bdst])
                yield
            for br in range(2):
                for j in range(2):
                    wb, Bwb = load_w(win_bf, BWIN, NTM + br * 1024 + j * 512, 512)
                    bk, Bbk = nb()
                    for c in range(8):
                        mm(bk[:], uT[:, c * 128:(c + 1) * 128], wb[:, c, :], c == 0, c == 7, [Bwb, BuT], [Bbk])
                    act(sig[:, j * 512:(j + 1) * 512], bk[:], AF.Sigmoid, [Bbk], [Bsig], waw=(j == 0))
                    yield
                srcT, BsrcT = (gattT, BgattT) if br == 0 else (grwT, BgrwT)
                wsrc_, Bwsrc_ = (wba_bf, BWBA) if br == 0 else (wbr_bf, BWBR)
                dst, Bdst = (t1, Bt1) if br == 0 else (tA, BtA)
                for j in range(2):
                    wb, Bwb = load_w(wsrc_, Bwsrc_, j * 512, 512)
                    bk, Bbk = nb()
                    for c in range(8):
                        mm(bk[:], srcT[:, c * 128:(c + 1) * 128], wb[:, c, :], c == 0, c == 7, [Bwb, BsrcT], [Bbk])
                    tt("dve", dst[:, j * 512:(j + 1) * 512], bk[:], sig[:, j * 512:(j + 1) * 512], ALU.mult,
                       [Bbk, Bsig], [Bdst], waw=(j == 0))
                    yield
            tt("pool", gatt[:], t1[:], tA[:], ALU.add, [Bt1, BtA], [Bgatt])
            bk, Bbk = nb()
            bkb = bk.bitcast(BF16)
            for c in range(8):
                tr(bkb[:, c * 128:(c + 1) * 128], gatt[:, c * 128:(c + 1) * 128], [Bgatt], [Bbk])
            cp("act", mrgT[:], bkb[:, 0:1024], [Bbk], [BmrgT])
            yield
            for j in range(2):
                wb, Bwb = load_w(wout_bf, BWOUT, j * 512, 512)
                bk, Bbk = nb()
                for c in range(8):
                    mm(bk[:], mrgT[:, c * 128:(c + 1) * 128], wb[:, c, :], c == 0, c == 7, [BmrgT, Bwb], [Bbk])
                tt("dve", tB[:, j * 512:(j + 1) * 512], bk[:], x_t[:, j * 512:(j + 1) * 512], ALU.add,
                   [Bbk, Bx], [BtB], waw=False)
                yield
            o_t, Bo = osb[0]
            fnw_ap, Bfnw = prm("fnw")
            rmsnorm_scale(tB[:], BtB, fnw_ap, Bfnw, o_t[:], Bo, so=80, Bs=Bsm_fn)
            P.dma(y[(b - 1) * 128:b * 128, :], o_t[:], Bo, reads=[Bo], writes=[BY])


        def drain(g):
            for _ in g:
                pass

        drain(head_gen(0))
        for b in range(NBLK):
            mid(b)
            bg = back_gen(b) if b >= 1 else None
            hg = head_gen(b + 1) if b + 1 < NBLK else None
            if bg is not None:
                hsteps = 0
                for v_ in bg:
                    if v_ == "gn":
                        break
                    if hg is not None and hsteps < 9:
                        for _ in range(3):
                            if hsteps < 9:
                                next(hg)
                                hsteps += 1
            gens = [g for g in (bg, hg) if g is not None]
            while gens:
                for g in list(gens):
                    try:
                        next(g)
                    except StopIteration:
                        gens.remove(g)

        P.wait_all("sp", [BY])
        P.E["sp"].ops.append(lambda e: e.nop())

        with nc.Block() as block:
            @block.sync
            def _(e):
                for f in P.E["sp"].ops:
                    f(e)

            @block.tensor
            def _(e):
                for f in P.E["pe"].ops:
                    f(e)

            @block.scalar
            def _(e):
                for f in P.E["act"].ops:
                    f(e)

            @block.vector
            def _(e):
                for f in P.E["dve"].ops:
                    f(e)

            @block.gpsimd
            def _(e):
                for f in P.E["pool"].ops:
                    f(e)
        build.stats = {k: len(v.ops) for k, v in P.E.items()}
        build.nsem = P.nsem
        build.sbuf_free = nc.sbuf_bytes_remaining() if callable(getattr(nc, "sbuf_bytes_remaining", None)) else getattr(nc, "sbuf_bytes_remaining", None)
    return nc


def make_consts():
    i = np.arange(128)
    ident = np.eye(128, dtype=np.float32)
    mc = (i[:, None] <= i[None, :]).astype(np.float32)
    mp = (i[:, None] > i[None, :]).astype(np.float32)
    mp1 = mp * (i[:, None] >= PAD).astype(np.float32)
    su = (i[:, None] < i[None, :]).astype(np.float32)
    sl = (i[:, None] > i[None, :]).astype(np.float32)
    maskA = np.concatenate([mc, mp, mc, mp], 1)
    maskA1 = np.concatenate([mc, mp1, mc, mp1], 1)
    SU4 = np.tile(su, (1, 4))
    IU4 = np.tile(mc, (1, 4))
    SL4 = np.tile(sl, (1, 4))
    tri = mc
    ones = np.ones((128, 128), np.float32)
    sh = (i[:, None] + 1 == i[None, :]).astype(np.float32)
    em = np.zeros((128, 128), np.float32)
    em[127, 0] = 1.0
    c = np.concatenate([ident, maskA, maskA1, SU4, IU4, SL4, tri, ones, sh, em], 1).astype(np.float32)
    assert c.shape == (128, NCONST)
    return np.ascontiguousarray(c)


def make_tabs(NBLK):
    half = 32
    inv = (1.0 / (np.float32(10000.0) ** (np.arange(half, dtype=np.float32) / np.float32(half)))).astype(np.float32)
    pos = (np.arange(NBLK * 128, dtype=np.float32) - np.float32(PAD)).astype(np.float32)
    ang = (pos[:, None] * inv[None, :]).astype(np.float32)
    return np.ascontiguousarray(np.concatenate([np.cos(ang), np.sin(ang)], 1).astype(np.float32))


_CACHE = {}


def run(inputs, NBLK, ncores):
    f = lambda a: np.ascontiguousarray(np.asarray(a, dtype=np.float32))
    x = f(inputs["x"])
    meta = f(inputs["meta_tokens"])
    nx = (NBLK - 1) * 128
    if NBLK not in _CACHE:
        _CACHE[NBLK] = build(NBLK)
    nc = _CACHE[NBLK]
    common = {
        "w_in": f(inputs["w_in"][0]), "w_ba": f(inputs["w_branch_att"][0]), "w_br": f(inputs["w_branch_rwkv"][0]),
        "w_out": f(inputs["w_out"][0]), "consts": make_consts(), "tabs": make_tabs(NBLK),
        "p_normw": f(inputs["norm_w"][0]).reshape(1, -1), "p_fnw": f(inputs["final_norm_w"]).reshape(1, -1),
        "p_mu": f(inputs["rk_mu"][0]).reshape(1, -1), "p_w0": f(inputs["rk_w0"][0]).reshape(1, -1),
        "p_a0": f(inputs["rk_a0"][0]).reshape(1, -1), "p_kk": f(inputs["rk_k_k"][0]).reshape(1, -1),
        "p_ka": f(inputs["rk_k_a"][0]).reshape(1, -1), "p_rk": f(inputs["rk_r_k"][0]).reshape(1, -1),
        "p_lnw": f(inputs["rk_ln_w"][0]).reshape(1, -1), "p_lnb": f(inputs["rk_ln_b"][0]).reshape(1, -1),
        "p_sinks": f(inputs["att_sinks"][0]).reshape(1, -1),
        "w2": f(inputs["rk_w2"][0]), "a2": f(inputs["rk_a2"][0]),
    }
    in_maps = []
    for cidx in range(ncores):
        h0 = np.zeros((NBLK * 128, D), np.float32)
        h0[PAD:128] = meta
        h0[128:] = x[cidx, :nx]
        m = dict(common)
        m["h0"] = h0
        in_maps.append(m)
    res = run_bass_kernel_spmd(nc, in_maps, core_ids=list(range(ncores)))
    return np.stack([np.asarray(r["y"], dtype=np.float32) for r in res.results], 0)


def kernel(**inputs):
    return run(inputs, 65, 8)
```

```python
import contextlib
import numpy as np
import concourse.bass as bass
import concourse.mybir as mybir
from concourse.bass_utils import run_bass_kernel_spmd

F32 = mybir.dt.float32
BF16 = mybir.dt.bfloat16
AF = mybir.ActivationFunctionType
ALU = mybir.AluOpType
AX = mybir.AxisListType

D = 1024
NIN = 8576
NTM = 6528
RW0 = 2304
RWN = 3200
GR0 = 5504
N_META = 16
PAD = 112
RMS_EPS = 1e-6
GN_EPS = 64e-5
NCONST = 128 + 5 * 512 + 512


class Buf:
    def __init__(self, name):
        self.name = name
        self.w = {}
        self.r = {}
        self.dsem = None
        self.dkey = None
        self.dcnt = 0
        self.subs = None

    def split(self, n):
        self.subs = [Buf("%s_s%d" % (self.name, i)) for i in range(n)]
        for sbuf in self.subs:
            sbuf.w = dict(self.w)
            sbuf.r = dict(self.r)
        return self


def expand(lst):
    out = []
    for b in lst:
        if isinstance(b, tuple):
            out.append(b[0].subs[b[1]] if b[0].subs else b[0])
        elif b.subs:
            out.extend(b.subs)
        else:
            out.append(b)
    return out


class Eng:
    def __init__(self, key, getter):
        self.key = key
        self.getter = getter
        self.cnt = 0
        self.seen = {}
        self.ops = []
        self.sem = None


class Prog:
    def __init__(self, nc, stack):
        self.nc = nc
        self.stack = stack
        self.sems = {}
        self.E = {}
        for key in ("pe", "act", "dve", "pool", "sp"):
            e = Eng(key, None)
            e.sem = stack.enter_context(nc.semaphore("s_" + key))
            self.sems[key] = e.sem
            self.E[key] = e
        self.nsem = 5

    def _deps(self, E, reads, writes, acc, waw):
        deps = {}

        def add(k, v):
            if deps.get(k, 0) < v:
                deps[k] = v
        for b in reads:
            for k, v in b.w.items():
                add(k, v)
            if getattr(b, "is_bank", False):
                for k, v in b.r.items():
                    if k != E.key:
                        add(k, v)
        for b in writes:
            if waw:
                for k, v in b.w.items():
                    if acc and k == E.key:
                        continue
                    add(k, v)
            for k, v in b.r.items():
                add(k, v)
        for k, v in deps.items():
            if E.seen.get(k, 0) < v:
                E.seen[k] = v
                sem = self.sems[k]
                E.ops.append(lambda e, sem=sem, v=v: e.wait_ge(sem, v))

    def op(self, ek, fn, reads=(), writes=(), acc=False, waw=True):
        E = self.E[ek]
        reads = expand(reads)
        writes = expand(writes)
        if ek != "pe":
            for b in reads:
                if getattr(b, "held", False):
                    b.held = False
        self._deps(E, reads, writes, acc, waw)
        E.cnt += 1
        cnt = E.cnt
        sem = E.sem
        E.ops.append(lambda e, fn=fn, sem=sem: fn(e).then_inc(sem, 1))
        for b in reads:
            b.r[E.key] = cnt
        for b in writes:
            b.w[E.key] = cnt

    def dma(self, out_ap, in_ap, sembuf, reads=(), writes=(), qk="sp"):
        Q = self.E[qk]
        reads = expand(reads)
        writes = expand(writes)
        self._deps(Q, reads, writes, False, True)
        sb = sembuf
        if sb.dsem is None:
            sb.dkey = "d_" + sb.name
            sb.dsem = self.stack.enter_context(self.nc.semaphore(sb.dkey))
            self.sems[sb.dkey] = sb.dsem
            self.nsem += 1
        if sb.dcnt > 0 and Q.seen.get(sb.dkey, 0) < sb.dcnt:
            Q.seen[sb.dkey] = sb.dcnt
            Q.ops.append(lambda e, sem=sb.dsem, v=sb.dcnt: e.wait_ge(sem, v))
        sb.dcnt += 16
        val = sb.dcnt
        dsem = sb.dsem
        Q.ops.append(lambda e, o=out_ap, i=in_ap, dsem=dsem: e.dma_start(out=o, in_=i).then_inc(dsem, 16))
        for b in reads:
            b.r[sb.dkey] = val
        for b in writes:
            b.w[sb.dkey] = val

    def wait_all(self, ek, bufs):
        E = self.E[ek]
        for b in expand(bufs):
            for k, v in list(b.w.items()) + list(b.r.items()):
                if E.seen.get(k, 0) < v:
                    E.seen[k] = v
                    sem = self.sems[k]
                    E.ops.append(lambda e, sem=sem, v=v: e.wait_ge(sem, v))


def bc_last(ap, n):
    return bass.AP(ap.tensor, ap.offset, [list(x) for x in ap.ap] + [[0, n]])


def bc_mid(ap, n):
    a = [list(x) for x in ap.ap]
    return bass.AP(ap.tensor, ap.offset, [a[0], [0, n]] + a[1:])


def build(NBLK):
    nc = bass.Bass("TRN2", target_bir_lowering=False)
    NOUT = NBLK - 1

    def din(name, shape):
        return nc.dram_tensor(name, list(shape), F32, kind="ExternalInput").ap()
    h0 = din("h0", [NBLK * 128, D])
    w_in = din("w_in", [D, NIN])
    w_ba = din("w_ba", [D, D])
    w_br = din("w_br", [D, D])
    w_out = din("w_out", [D, D])
    consts = din("consts", [128, NCONST])
    tabs = din("tabs", [NBLK * 128, 64])
    pvec = {}
    for nm, n in (("normw", D), ("fnw", D), ("mu", RWN), ("w0", D), ("a0", D), ("kk", D), ("ka", D),
                  ("rk", D), ("lnw", D), ("lnb", D), ("sinks", 16)):
        pvec[nm] = din("p_" + nm, [1, n])
    w2d = din("w2", [64, D])
    a2d = din("a2", [64, D])
    y = nc.dram_tensor("y", [NOUT * 128, D], F32, kind="ExternalOutput").ap()
    win_bf = nc.dram_tensor("win_bf", [8, 128, NIN], BF16, kind="Internal").ap()
    wba_bf = nc.dram_tensor("wba_bf", [8, 128, D], BF16, kind="Internal").ap()
    wbr_bf = nc.dram_tensor("wbr_bf", [8, 128, D], BF16, kind="Internal").ap()
    wout_bf = nc.dram_tensor("wout_bf", [8, 128, D], BF16, kind="Internal").ap()

    with contextlib.ExitStack() as stack:
        P = Prog(nc, stack)
        bufs = {}

        def sb(name, shape, dt=F32):
            t = stack.enter_context(nc.sbuf_tensor(name, list(shape), dt))
            b = Buf(name)
            bufs[name] = b
            return t, b

        def B(name):
            b = Buf(name)
            return b

        cstb, Bcstb = sb("cstb", [128, NCONST], BF16)
        cstf, Bcstf = sb("cstf", [128, 512])
        pslot = [sb("pslot%d" % i, [128, D]) for i in range(3)]
        esk, Besk = sb("esk", [128, 16])
        w2b, Bw2b = sb("w2b", [64, 2 * D], BF16)
        xs = [sb("xs%d" % i, [128, D]) for i in range(2)]
        tab = [sb("tab%d" % i, [128, 64]) for i in range(2)]
        NW = 4
        wbuf = [sb("wbuf%d" % i, [128, 8, 512], BF16) for i in range(NW)]
        u_bf, Bu = sb("u_bf", [128, D], BF16)
        uTs2 = [sb("uT%d" % i, [128, D], BF16) for i in range(2)]
        uTs_t, BuTs = sb("uTs", [128, D], BF16)
        ulast, Bulast = sb("ulast", [128, 8], BF16)
        junkH, BjunkH = sb("junkH", [128, D])
        shtmp = [sb("shtmp%d" % i, [128, 512]) for i in range(2)]
        Bsm_fn = B("sm_fn")
        p_sb, Bp_all = sb("p_sb", [128, NTM])
        Bp_att, Bp_rw, Bp_gr = B("p_att"), B("p_rw"), B("p_gr")
        small, Bsmall = sb("small", [128, 96])
        Bsm_att, Bsm_kk, Bsm_gn, Bsm_bon = B("sm_att"), B("sm_kk"), B("sm_gn"), B("sm_bon")
        tA, BtA = sb("tA", [128, D])
        tB, BtB = sb("tB", [128, D])
        tC, BtC = sb("tC", [128, D])
        tD, BtD = sb("tD", [128, D])
        tE, BtE = sb("tE", [128, D])
        tF, BtF = sb("tF", [128, D])
        tG, BtG = sb("tG", [128, D])
        tH, BtH = sb("tH", [128, D])
        q_rot, Bq_rot = sb("q_rot", [128, D], BF16)
        k_rot, Bk_rot = sb("k_rot", [128, 128], BF16)
        qT, BqT = sb("qT", [64, 16 * 128], BF16)
        kT = [sb("kT%d" % i, [64, 256], BF16) for i in range(2)]
        vaug = [sb("vaug%d" % i, [128, 2, 65], BF16) for i in range(2)]
        pT = [sb("pT%d" % i, [128, 512], BF16) for i in range(2)]
        y_sb, By = sb("y_sb", [128, D])
        att, Batt = p_sb, Bp_att
        gatt, Bgatt = sb("gatt", [128, D], BF16)
        grw, Bgrw = sb("grw", [128, D], BF16)
        gattT, BgattT = sb("gattT", [128, D], BF16)
        grwT, BgrwT = sb("grwT", [128, D], BF16)
        mrgT, BmrgT = sb("mrgT", [128, D], BF16)
        lora_bf, Blora = sb("lora_bf", [128, 128], BF16)
        loraT, BloraT = sb("loraT", [64, 256], BF16)
        sig, Bsig = sb("sig", [128, D])
        t1, Bt1 = sb("t1", [128, D])
        junk, Bjunk = t1, Bt1
        tl = {"r_t": q_rot, "a_t": gattT, "b_t": grwT, "k_t": mrgT}
        Btl = {"r_t": Bq_rot, "a_t": BgattT, "b_t": BgrwT, "k_t": BmrgT}
        for nm in ("b_h", "k_h", "v_b"):
            tl[nm], Btl[nm] = sb("tl_" + nm, [128, D], BF16)
        fm = {"rT": sig.bitcast(BF16)[0:64, :], "aT": t1.bitcast(BF16)[0:64, :],
              "bT": tD.bitcast(BF16)[0:64, :], "kT": tH.bitcast(BF16)[0:64, :]}
        Bfm = {"rT": Bsig, "aT": Bt1, "bT": BtD, "kT": BtH}
        NS = 2
        mats = []
        for s_ in range(NS):
            d_ = {}
            for nm in ("N0", "N1", "L0", "L1", "AK", "RB", "RK"):
                d_[nm] = sb("m%d_%s" % (s_, nm), [128, 512], BF16)
            d_["Xf"] = sb("m%d_Xf" % s_, [128, 256])
            d_["Xb"] = sb("m%d_Xb" % s_, [128, 256], BF16)
            mats.append(d_)
        Hf, _ = sb("Hf", [64, D])
        Hb, _ = sb("Hb", [64, D], BF16)
        BHf = [B("Hf%d" % i) for i in range(4)]
        BHb = [B("Hb%d" % i) for i in range(4)]
        tmpH, BtmpH = sb("tmpH", [64, 256])
        pcfm, Bpcfm = sb("pcfm", [64, 16])
        osb = [sb("osb%d" % i, [128, D]) for i in range(1)]
        carry = nc.dram_tensor("carry", [2, RWN], F32, kind="Internal").ap()
        BCARRY = [B("carry0"), B("carry1")]
        Bps0 = B("ps_row0")
        cst = p_sb[:, 0:NCONST]
        Bcst = Bp_all
        w2f = p_sb[0:64, NCONST:NCONST + 2 * D]
        Bw2f = B("w2f")
        stg_f = [(tB, B("stgf0")), (tC, B("stgf1"))]
        sigb = sig.bitcast(BF16)
        stg_b = [(sigb[:, 0:1024], B("stgb0")), (tA.bitcast(BF16)[:, 0:1024], B("stgb1"))]

        for bb_ in (BtA, BtB, BtC, BtD, BtE, BtF, BtG, BtH, Bq_rot, BgattT, BgrwT, BmrgT, Btl["b_h"], Btl["k_h"],
                    Btl["v_b"], Bsig, Bt1, Bpcfm):
            bb_.split(2)
        Bsm_kk2 = [B("sm_kk0"), B("sm_kk1")]

        banks = []
        for i in range(8):
            t = stack.enter_context(nc.psum_tensor("bank%d" % i, [128, 512], F32))
            bb_ = B("bank%d" % i)
            bb_.is_bank = True
            banks.append((t, bb_))
        bank_i = [0]

        def nb():
            for _ in range(8):
                t = banks[bank_i[0] % 8]
                bank_i[0] += 1
                if not getattr(t[1], "held", False):
                    t[1].held = True
                    return t
            raise RuntimeError("no free PSUM bank")

        BWIN, BWBA, BWBR, BWOUT = B("win"), B("wba"), B("wbr"), B("wout")
        BY = B("yout")

        ident = cstb[:, 0:128]
        maskA = cstb[:, 128:640]
        maskA1 = cstb[:, 640:1152]
        SU4 = cstb[:, 1152:1664]
        IU4 = cstb[:, 1664:2176]
        SL4 = cstb[:, 2176:2688]
        tri_f = cstf[:, 0:128]
        ones_f = cstf[:, 128:256]
        sh_f = cstf[:, 256:384]
        e_f = cstf[:, 384:512]

        def mm(out, lhsT, rhs, start, stop, reads, writes):
            P.op("pe", lambda e: e.matmul(out, lhsT, rhs, start=start, stop=stop), reads=reads, writes=writes,
                 acc=True)

        def tr(out, in_, reads, writes, np_=128):
            P.op("pe", lambda e: e.transpose(out, in_, ident[0:np_, 0:np_]), reads=list(reads) + [Bcstb],
                 writes=writes, acc=True)

        def act(out, in_, func, reads, writes, bias=None, scale=None, waw=True):
            kw = {}
            if bias is not None:
                kw["bias"] = bias
            if scale is not None:
                kw["scale"] = scale
            P.op("act", lambda e: e.activation(out, in_, func, **kw), reads=reads, writes=writes, waw=waw)

        def tt(ek, out, in0, in1, op, reads, writes, waw=True):
            P.op(ek, lambda e: e.tensor_tensor(out, in0, in1, op), reads=reads, writes=writes, waw=waw)

        def ts(ek, out, in0, s1, s2, op0, op1, reads, writes):
            if op1 is None:
                P.op(ek, lambda e: e.tensor_scalar(out, in0, s1, None, op0), reads=reads, writes=writes)
            else:
                P.op(ek, lambda e: e.tensor_scalar(out, in0, s1, s2, op0, op1), reads=reads, writes=writes)

        def stt(out, in0, scalar, in1, op0, op1, reads, writes):
            P.op("dve", lambda e: e.scalar_tensor_tensor(out, in0, scalar, in1, op0, op1), reads=reads,
                 writes=writes)

        def red(out, in_, reads, writes):
            P.op("dve", lambda e: e.tensor_reduce(out, in_, AX.X, ALU.add), reads=reads, writes=writes)

        def recip(out, in_, reads, writes):
            P.op("dve", lambda e: e.reciprocal(out, in_), reads=reads, writes=writes)

        def cp(ek, out, in_, reads, writes, waw=True):
            if ek == "act":
                act(out, in_, AF.Copy, reads, writes, waw=waw)
            else:
                P.op(ek, lambda e: e.tensor_copy(out, in_), reads=reads, writes=writes, waw=waw)

        def memset(ek, ap, val, writes):
            P.op(ek, lambda e: e.memset(ap, val), writes=writes)

        pcnt = [0]

        def prm(nm, c0=0, n=D):
            t, Bt = pslot[pcnt[0] % 3]
            pcnt[0] += 1
            src = bass.AP(pvec[nm].tensor, c0, [[0, 128], [1, n]])
            P.dma(t[:, 0:n], src, Bt, writes=[Bt])
            return t[:, 0:n], Bt

        def strided_cols(ap1, step, n):
            a = [list(x) for x in ap1.ap]
            return bass.AP(ap1.tensor, ap1.offset, [a[0], [step, n]])

        def merge(dst, src):
            for d1 in expand([dst]):
                for dd, sd in ((d1.r, src.w), (d1.r, src.r)):
                    for k, v in sd.items():
                        if dd.get(k, 0) < v:
                            dd[k] = v

        P.dma(cst, consts[:, :], Bcst, writes=[Bcst])
        cp("dve", cstb[:], cst, [Bcst], [Bcstb])
        cp("pool", cstf[:], p_sb[:, 2688:3200], [Bcst], [Bcstf])
        P.dma(esk[:], bass.AP(pvec["sinks"].tensor, 0, [[0, 128], [1, 16]]), Besk, writes=[Besk])
        act(esk[:], esk[:], AF.Exp, [Besk], [Besk])
        P.dma(w2f[:, 0:D], w2d[:, :], Bw2f, writes=[Bw2f])
        P.dma(w2f[:, D:2 * D], a2d[:, :], B("w2f_b"), writes=[Bw2f])
        cp("dve", w2b[:], w2f, [Bw2f], [Bw2b])
        si = 0
        cast_engs = ["dve", "pool", "act"]
        for (wsrc, wdst, Bd, ncol) in ((w_in, win_bf, BWIN, NIN), (w_ba, wba_bf, BWBA, D), (w_br, wbr_bf, BWBR, D),
                                       (w_out, wout_bf, BWOUT, D)):
            pw = 536 if ncol == NIN else 1024
            for c in range(8):
                for c0 in range(0, ncol, pw):
                    sf, Bsf = stg_f[si % 2]
                    sbf, Bsb = stg_b[si % 2]
                    P.dma(sf[:, 0:pw], wsrc[c * 128:(c + 1) * 128, c0:c0 + pw], Bsf, writes=[Bsf])
                    cp(cast_engs[si % 3], sbf[:, 0:pw], sf[:, 0:pw], [Bsf], [Bsb])
                    P.dma(wdst[c, :, c0:c0 + pw], sbf[:, 0:pw], Bsb, reads=[Bsb], writes=[Bd])
                    si += 1
        merge(BtB, stg_f[0][1])
        merge(BtC, stg_f[1][1])
        merge(Bsig, stg_b[0][1])
        merge(BtA, stg_b[1][1])
        for bb in (Bp_att, Bp_rw, Bp_gr):
            merge(bb, Bcst)
            merge(bb, Bw2f)
        memset("pool", ulast[:], 0.0, [Bulast])
        memset("pool", Hf[:], 0.0, BHf)
        memset("pool", Hb[:], 0.0, BHb)
        for i in range(2):
            memset("pool", vaug[i][0][:], 1.0, [vaug[i][1]])

        def load_x(b):
            par = b % 2
            P.dma(xs[par][0][:], h0[b * 128:(b + 1) * 128, :], xs[par][1], writes=[xs[par][1]])
            P.dma(tab[par][0][:], tabs[b * 128:(b + 1) * 128, :], tab[par][1], writes=[tab[par][1]])

        load_x(0)
        if NBLK > 1:
            load_x(1)
        wcnt = [0]

        def load_w(src, Bsrc, c0, ncols):
            wb, Bwb = wbuf[wcnt[0] % NW]
            wcnt[0] += 1
            P.dma(wb[:, :, 0:ncols], src[:, :, c0:c0 + ncols].rearrange("c p n -> p c n"), Bwb, reads=[Bsrc],
                  writes=[Bwb])
            return wb, Bwb

        def rmsnorm_scale(src, Bsrc, wrep, Bwrep, out, Bout, jk=None, Bjk=None, so=0, Bs=None):
            jk = junk if jk is None else jk
            Bjk = Bjunk if Bjk is None else Bjk
            Bs = Bsmall if Bs is None else Bs
            act(jk[:], src, AF.Square, [Bsrc], [Bjk])
            red(small[:, so:so + 1], jk[:], [Bjk], [Bs])
            ts("dve", small[:, so + 1:so + 2], small[:, so:so + 1], 1.0 / D, RMS_EPS, ALU.mult, ALU.add, [Bs], [Bs])
            act(small[:, so + 2:so + 3], small[:, so + 1:so + 2], AF.Ln, [Bs], [Bs])
            act(small[:, so + 3:so + 4], small[:, so + 2:so + 3], AF.Exp, [Bs], [Bs], scale=-0.5)
            stt(out, src, small[:, so + 3:so + 4], wrep, ALU.mult, ALU.mult, [Bsrc, Bs, Bwrep], [Bout])

        def head_gen(b):
            par = b % 2
            x_t, Bx = xs[par]
            uT, BuT = uTs2[par]
            nw_ap, Bnw = prm("normw")
            rmsnorm_scale(x_t[:], Bx, nw_ap, Bnw, u_bf[:], Bu, jk=junkH, Bjk=BjunkH)
            bk, Bbk = nb()
            bkb = bk.bitcast(BF16)
            for c in range(8):
                tr(bkb[:, c * 128:(c + 1) * 128], u_bf[:, c * 128:(c + 1) * 128], [Bu], [Bbk])
            cp("act", uT[:], bkb[:, 0:1024], [Bbk], [BuT])
            uT3 = uT[:].rearrange("p (c t) -> p c t", c=8, t=128)
            uS3 = uTs_t[:].rearrange("p (c t) -> p c t", c=8, t=128)
            cp("pool", uS3[:, :, 1:128], uT3[:, :, 0:127], [BuT], [BuTs])
            cp("pool", uS3[:, :, 0:1], ulast[:].rearrange("p (c o) -> p c o", c=8, o=1), [Bulast], [BuTs], waw=False)
            cp("pool", ulast[:].rearrange("p (c o) -> p c o", c=8, o=1), uT3[:, :, 127:128], [BuT], [Bulast])
            yield
            for nt in range(13):
                n0 = nt * 512
                nsz = min(512, NTM - n0)
                wb, Bwb = load_w(win_bf, BWIN, n0, nsz)
                bk, Bbk = nb()
                for c in range(8):
                    mm(bk[:, 0:nsz], uT[:, c * 128:(c + 1) * 128], wb[:, c, 0:nsz], c == 0, c == 7,
                       [BuT, Bwb], [Bbk])
                r0, r1 = max(n0, RW0), min(n0 + nsz, GR0)
                if r1 > r0:
                    bk2, Bbk2 = nb()
                    for c in range(8):
                        mm(bk2[:, r0 - n0:r1 - n0], uTs_t[:, c * 128:(c + 1) * 128], wb[:, c, r0 - n0:r1 - n0],
                           c == 0, c == 7, [BuTs, Bwb], [Bbk2])
                segs = ((0, 1280, False, Bp_att), (1280, RW0, True, Bp_att), (RW0, GR0, False, Bp_rw),
                        (GR0, NTM, True, Bp_gr))
                has_silu = any(sg[2] and sg[0] < n0 + nsz and sg[1] > n0 for sg in segs)
                for (s0, s1, is_silu, Bseg) in segs:
                    a0, a1 = max(n0, s0), min(n0 + nsz, s1)
                    if a1 <= a0:
                        continue
                    if is_silu:
                        act(p_sb[:, a0:a1], bk[:, a0 - n0:a1 - n0], AF.Silu, [Bbk], [Bseg], waw=False)
                    else:
                        eng = "act" if (has_silu or nt % 2 == 0) else "dve"
                        cp(eng, p_sb[:, a0:a1], bk[:, a0 - n0:a1 - n0], [Bbk], [Bseg], waw=False)
                if r1 > r0:
                    n_ = r1 - r0
                    st_t, Bst = shtmp[nt % 2]
                    mu_ap, Bmu = prm("mu", r0 - RW0, n_)
                    tt("dve", st_t[:, 0:n_], bk2[:, r0 - n0:r1 - n0], p_sb[:, r0:r1], ALU.subtract,
                       [Bbk2, Bp_rw], [Bst])
                    tt("pool", st_t[:, 0:n_], st_t[:, 0:n_], mu_ap, ALU.mult, [Bst, Bmu], [Bst])
                    tt("dve" if nt % 2 == 0 else "pool", p_sb[:, r0:r1], st_t[:, 0:n_], p_sb[:, r0:r1], ALU.add,
                       [Bst, Bp_rw], [Bp_rw], waw=False)
                yield

        def mid(b):
            par = b % 2
            x_t, Bx = xs[par]
            tb_t, Btb = tab[par]
            uT, BuT = uTs2[par]
            if b >= 1 and b + 1 < NBLK:
                load_x(b + 1)

            def v3(ap, a, bdim):
                return ap.rearrange("p (a b) -> p a b", a=a, b=bdim)
            q3 = v3(p_sb[:, 0:1024], 16, 64)
            qr3 = v3(q_rot[:], 16, 64)
            cosb = bc_mid(tb_t[:, 0:32], 16)
            sinb = bc_mid(tb_t[:, 32:64], 16)
            A3 = v3(tA[:, 0:512], 16, 32)
            B3 = v3(tB[:, 0:512], 16, 32)
            C3 = v3(tA[:, 512:1024], 16, 32)
            D3 = v3(tB[:, 512:1024], 16, 32)
            tt("dve", A3, q3[:, :, 0:32], cosb, ALU.mult, [Bp_att, Btb], [BtA])
            tt("pool", B3, q3[:, :, 32:64], sinb, ALU.mult, [Bp_att, Btb], [BtB])
            tt("dve", qr3[:, :, 0:32], A3, B3, ALU.subtract, [BtA, BtB], [Bq_rot])
            tt("pool", C3, q3[:, :, 32:64], cosb, ALU.mult, [Bp_att, Btb], [BtA])
            tt("dve", D3, q3[:, :, 0:32], sinb, ALU.mult, [Bp_att, Btb], [BtB])
            tt("pool", qr3[:, :, 32:64], C3, D3, ALU.add, [BtA, BtB], [Bq_rot])
            k3 = v3(p_sb[:, 1024:1152], 2, 64)
            kr3 = v3(k_rot[:], 2, 64)
            cos2 = bc_mid(tb_t[:, 0:32], 2)
            sin2 = bc_mid(tb_t[:, 32:64], 2)
            E3 = v3(tC[:, 0:64], 2, 32)
            F3 = v3(tC[:, 64:128], 2, 32)
            G3 = v3(tC[:, 128:192], 2, 32)
            H3 = v3(tC[:, 192:256], 2, 32)
            tt("dve", E3, k3[:, :, 0:32], cos2, ALU.mult, [Bp_att, Btb], [BtC])
            tt("dve", F3, k3[:, :, 32:64], sin2, ALU.mult, [Bp_att, Btb], [BtC])
            tt("dve", kr3[:, :, 0:32], E3, F3, ALU.subtract, [BtC], [Bk_rot])
            tt("dve", G3, k3[:, :, 32:64], cos2, ALU.mult, [Bp_att, Btb], [BtC])
            tt("dve", H3, k3[:, :, 0:32], sin2, ALU.mult, [Bp_att, Btb], [BtC])
            tt("dve", kr3[:, :, 32:64], G3, H3, ALU.add, [BtC], [Bk_rot])
            va_t, Bva = vaug[par]
            cp("pool", va_t[:, :, 0:64], v3(p_sb[:, 1152:1280], 2, 64), [Bp_att], [Bva])
            kT_t, BkT = kT[par]
            bk, Bbk = nb()
            bkb = bk.bitcast(BF16)
            for g in range(2):
                tr(bkb[0:64, g * 128:(g + 1) * 128], k_rot[:, g * 64:(g + 1) * 64], [Bk_rot], [Bbk])
            cp("dve", kT_t[:], bkb[0:64, 0:256], [Bbk], [BkT])
            if b >= 1:
                for hh in range(2):
                    bk, Bbk = nb()
                    bkb = bk.bitcast(BF16)
                    for i in range(8):
                        h = hh * 8 + i
                        tr(bkb[0:64, i * 128:(i + 1) * 128], q_rot[:, h * 64:(h + 1) * 64], [Bq_rot], [Bbk])
                    cp("act" if hh == 0 else "dve", qT[:, hh * 1024:(hh + 1) * 1024], bkb[0:64, 0:1024], [Bbk],
                       [BqT], waw=False)

            r_ = p_sb[:, RW0:RW0 + 1024]
            k_ = p_sb[:, RW0 + 1024:RW0 + 2048]
            v_ = p_sb[:, RW0 + 2048:RW0 + 3072]
            wl_ = p_sb[:, RW0 + 3072:RW0 + 3136]
            al_ = p_sb[:, RW0 + 3136:RW0 + 3200]
            act(lora_bf[:, 0:64], wl_, AF.Tanh, [Bp_rw], [Blora])
            cp("dve", lora_bf[:, 64:128], al_, [Bp_rw], [Blora])
            bk, Bbk = nb()
            bkb = bk.bitcast(BF16)
            tr(bkb[0:64, 0:128], lora_bf[:, 0:64], [Blora], [Bbk])
            tr(bkb[0:64, 128:256], lora_bf[:, 64:128], [Blora], [Bbk])
            cp("dve", loraT[:], bkb[0:64, 0:256], [Bbk], [BloraT])
            def prep_half(hf):
                cs = slice(hf * 512, (hf + 1) * 512)

                def R(bf):
                    return (bf, hf)
                w0_ap, Bw0 = prm("w0", hf * 512, 512)
                bk, Bbk = nb()
                mm(bk[:], loraT[:, 0:128], w2b[:, cs], True, True, [BloraT, Bw2b], [Bbk])
                tt("dve", tA[:, cs], bk[:], w0_ap, ALU.add, [Bbk, Bw0], [R(BtA)])
                yield
                act(tA[:, cs], tA[:, cs], AF.Exp, [R(BtA)], [R(BtA)], scale=-1.0)
                act(tA[:, cs], tA[:, cs], AF.Ln, [R(BtA)], [R(BtA)], bias=1.0)
                act(tA[:, cs], tA[:, cs], AF.Exp, [R(BtA)], [R(BtA)], scale=-1.0, bias=-0.5)
                yield
                a0_ap, Ba0 = prm("a0", hf * 512, 512)
                bk, Bbk = nb()
                mm(bk[:], loraT[:, 128:256], w2b[:, D + hf * 512:D + (hf + 1) * 512], True, True, [BloraT, Bw2b],
                   [Bbk])
                tt("dve", tB[:, cs], bk[:], a0_ap, ALU.add, [Bbk, Ba0], [R(BtB)])
                yield
                act(tB[:, cs], tB[:, cs], AF.Sigmoid, [R(BtB)], [R(BtB)])
                bk, Bbk = nb()
                mm(bk[:], tri_f, tA[:, cs], True, True, [Bcstf, R(BtA)], [Bbk])
                cp("act", tC[:, cs], bk[:], [Bbk], [R(BtC)])
                bk2, Bbk2 = nb()
                mm(bk2[:], ones_f, tA[:, cs], True, True, [Bcstf, R(BtA)], [Bbk2])
                tt("dve", tE[:, cs], bk2[:], tC[:, cs], ALU.subtract, [Bbk2, R(BtC)], [R(BtE)])
                yield
                bk, Bbk = nb()
                for i in range(8):
                    h = hf * 8 + i
                    mm(bk[0:64, 2 * i:2 * i + 2], tA[:, h * 64:(h + 1) * 64], ones_f[:, 0:2], True, True,
                       [R(BtA), Bcstf], [Bbk])
                act(pcfm[:, hf * 8:(hf + 1) * 8], strided_cols(bk[0:64, 0:1], 2, 8), AF.Exp, [Bbk], [R(Bpcfm)],
                    scale=-1.0)
                yield
                kk_ap, Bkk = prm("kk", hf * 512, 512)
                tt("dve", tF[:, cs], k_[:, cs], kk_ap, ALU.mult, [Bp_rw, Bkk], [R(BtF)])
                tt("pool", tG[:, cs], tF[:, cs], tF[:, cs], ALU.mult, [R(BtF)], [R(BtG)])
                yield
                smk = small[:, 24 + 8 * hf:32 + 8 * hf]
                Bsk = Bsm_kk2[hf]
                red(smk, v3(tG[:, cs], 8, 64), [R(BtG)], [Bsk])
                act(smk, smk, AF.Sqrt, [Bsk], [Bsk])
                ts("dve", smk, smk, 1e-12, None, ALU.max, None, [Bsk], [Bsk])
                recip(smk, smk, [Bsk], [Bsk])
                tt("dve", v3(tF[:, cs], 8, 64), v3(tF[:, cs], 8, 64), bc_last(smk, 64), ALU.mult,
                   [R(BtF), Bsk], [R(BtF)])
                yield
                ka_ap, Bka = prm("ka", hf * 512, 512)
                stt(tG[:, cs], tB[:, cs], -1.0, ka_ap, ALU.add, ALU.mult, [R(BtB), Bka], [R(BtG)])
                stt(tG[:, cs], tG[:, cs], 1.0, k_[:, cs], ALU.add, ALU.mult, [R(BtG), Bp_rw], [R(BtG)])
                tt("pool", tB[:, cs], tF[:, cs], tB[:, cs], ALU.mult, [R(BtF), R(BtB)], [R(BtB)])
                yield
                act(tH[:, cs], tC[:, cs], AF.Exp, [R(BtC)], [R(BtH)], scale=-1.0)
                tt("dve", tl["r_t"][:, cs], r_[:, cs], tH[:, cs], ALU.mult, [Bp_rw, R(BtH)], [R(Btl["r_t"])])
                yield
                tt("pool", tH[:, cs], tC[:, cs], tA[:, cs], ALU.subtract, [R(BtC), R(BtA), R(Btl["r_t"])], [R(BtH)])
                act(tH[:, cs], tH[:, cs], AF.Exp, [R(BtH)], [R(BtH)], scale=-1.0)
                stt(tl["a_t"][:, cs], tF[:, cs], -1.0, tH[:, cs], ALU.mult, ALU.mult, [R(BtF), R(BtH)],
                    [R(Btl["a_t"])])
                yield
                act(tH[:, cs], tC[:, cs], AF.Exp, [R(BtC), R(Btl["a_t"])], [R(BtH)])
                tt("dve", tl["b_t"][:, cs], tB[:, cs], tH[:, cs], ALU.mult, [R(BtB), R(BtH)], [R(Btl["b_t"])])
                tt("pool", tl["k_t"][:, cs], tG[:, cs], tH[:, cs], ALU.mult, [R(BtG), R(BtH)], [R(Btl["k_t"])])
                yield
                act(tE[:, cs], tE[:, cs], AF.Exp, [R(BtE)], [R(BtE)], scale=-1.0)
                tt("dve", tl["b_h"][:, cs], tB[:, cs], tE[:, cs], ALU.mult, [R(BtB), R(BtE)], [R(Btl["b_h"])])
                tt("pool", tl["k_h"][:, cs], tG[:, cs], tE[:, cs], ALU.mult, [R(BtG), R(BtE)], [R(Btl["k_h"])])
                cp("act", tl["v_b"][:, cs], v_[:, cs], [Bp_rw], [R(Btl["v_b"])])
                yield
                for ti, (nm_t, nm_f) in enumerate((("r_t", "rT"), ("a_t", "aT"), ("b_t", "bT"), ("k_t", "kT"))):
                    bk, Bbk = nb()
                    bkb = bk.bitcast(BF16)
                    for i in range(8):
                        h = hf * 8 + i
                        tr(bkb[0:64, i * 128:(i + 1) * 128], tl[nm_t][:, h * 64:(h + 1) * 64], [R(Btl[nm_t])], [Bbk])
                    cp("act" if (ti + hf) % 2 == 0 else "dve", fm[nm_f][:, hf * 1024:(hf + 1) * 1024],
                       bkb[0:64, 0:1024], [Bbk], [R(Bfm[nm_f])])
                    yield

            pg = [prep_half(0), prep_half(1)]
            while pg:
                for g_ in list(pg):
                    try:
                        next(g_)
                    except StopIteration:
                        pg.remove(g_)
            rT, aT, bT, kTr = fm["rT"], fm["aT"], fm["bT"], fm["kT"]
            BrT, BaT, BbT, BkTr = Bfm["rT"], Bfm["aT"], Bfm["bT"], Bfm["kT"]
            vb = tl["v_b"]
            Bvb = Btl["v_b"]

            def attn_gen():
                kTp, BkTp = kT[1 - par]
                vap, Bvap = vaug[1 - par]
                msk = maskA1 if b == 1 else maskA
                for hq in range(4):
                    ob, Bob = nb()
                    for hp2 in range(2):
                        hp = hq * 2 + hp2
                        g = hp // 4
                        bk, Bbk = nb()
                        for i in range(2):
                            h = hp * 2 + i
                            mm(bk[:, (2 * i) * 128:(2 * i + 1) * 128], kT_t[:, g * 128:(g + 1) * 128],
                               qT[:, h * 128:(h + 1) * 128], True, True, [BkT, BqT], [Bbk])
                            mm(bk[:, (2 * i + 1) * 128:(2 * i + 2) * 128], kTp[:, g * 128:(g + 1) * 128],
                               qT[:, h * 128:(h + 1) * 128], True, True, [BkTp, BqT], [Bbk])
                        pT_t, BpT = pT[hp % 2]
                        act(pT_t[:], bk[:], AF.Exp, [Bbk], [BpT], scale=0.125)
                        yield
                        tt("pool", pT_t[:], pT_t[:], msk, ALU.mult, [BpT, Bcstb], [BpT])
                        yield
                        for i in range(2):
                            h = hp * 2 + i
                            j = hp2 * 2 + i
                            mm(ob[:, j * 65:(j + 1) * 65], pT_t[:, (2 * i) * 128:(2 * i + 1) * 128], va_t[:, g, :],
                               True, False, [BpT, Bva], [Bob])
                            mm(ob[:, j * 65:(j + 1) * 65], pT_t[:, (2 * i + 1) * 128:(2 * i + 2) * 128],
                               vap[:, g, :], False, True, [BpT, Bvap], [Bob])
                        yield
                    ob3 = ob[:, 0:260].rearrange("p (a b) -> p a b", a=4, b=65)
                    den = small[:, 8 + hq * 4: 12 + hq * 4]
                    tt("dve", den, strided_cols(ob[:, 64:65], 65, 4), esk[:, hq * 4:(hq + 1) * 4], ALU.add,
                       [Bob, Besk], [Bsm_att])
                    recip(den, den, [Bsm_att], [Bsm_att])
                    tt("dve", v3(att[:, hq * 256:(hq + 1) * 256], 4, 64), ob3[:, :, 0:64], bc_last(den, 64),
                       ALU.mult, [Bob, Bsm_att], [Batt], waw=False)
                    yield
                tt("pool", gatt[:], att[:, 0:1024], p_sb[:, 1280:2304], ALU.mult, [Batt], [Bgatt])
                yield

            def bonus_gen():
                tt("pool", tA[:], r_, tG[:], ALU.mult, [Bp_rw, BtG], [BtA])
                yield
                rk_ap, Brk = prm("rk")
                tt("pool", tA[:], tA[:], rk_ap, ALU.mult, [BtA, Brk], [BtA])
                yield
                red(small[:, 64:80], v3(tA[:], 16, 64), [BtA], [Bsm_bon])
                yield
                tt("pool", v3(v_, 16, 64), v3(v_, 16, 64), bc_last(small[:, 64:80], 64), ALU.mult,
                   [Bp_rw, Bsm_bon], [Bp_rw])
                yield

            def quad_gen(Q):
                M = mats[Q % NS]
                hs = [Q * 4 + i for i in range(4)]

                def hsl(h):
                    return slice(h * 128, (h + 1) * 128)

                def isl(i):
                    return slice(i * 128, (i + 1) * 128)
                specs = (("N0", bT, BbT, aT, BaT, SU4), ("L0", aT, BaT, bT, BbT, SL4),
                         ("AK", kTr, BkTr, aT, BaT, SU4), ("RB", bT, BbT, rT, BrT, IU4),
                         ("RK", kTr, BkTr, rT, BrT, IU4))
                for (nm, lt, Blt, rh, Brh, msk_) in specs:
                    if b == 0 and nm in ("RB", "RK"):
                        continue
                    bk, Bbk = nb()
                    for i, h in enumerate(hs):
                        mm(bk[:, isl(i)], lt[:, hsl(h)], rh[:, hsl(h)], True, True, [Blt, Brh], [Bbk])
                    if nm in ("RB", "RK"):
                        cp("act", M[nm][0][:], bk[:], [Bbk], [M[nm][1]])
                        tt("pool", M[nm][0][:], M[nm][0][:], msk_, ALU.mult, [M[nm][1], Bcstb], [M[nm][1]])
                    else:
                        tt("dve", M[nm][0][:], bk[:], msk_, ALU.mult, [Bbk, Bcstb], [M[nm][1]])
                yield
                Xf, BXf = M["Xf"]
                Xb, BXb = M["Xb"]
                bk, Bbk = nb()
                for i, h in enumerate(hs):
                    osl = slice(i * 64, (i + 1) * 64)
                    mm(bk[:, osl], aT[:, hsl(h)], Hb[:, h * 64:(h + 1) * 64], True, True, [BaT, BHb[Q]], [Bbk])
                for i, h in enumerate(hs):
                    osl2 = slice(256 + i * 64, 256 + (i + 1) * 64)
                    mm(bk[:, osl2], M["AK"][0][:, isl(i)], vb[:, h * 64:(h + 1) * 64], True, True,
                       [M["AK"][1], Bvb], [Bbk])
                cp("act", Xf[:], bk[:, 0:256], [Bbk], [BXf])
                tt("dve", Xf[:], bk[:, 256:512], Xf[:], ALU.add, [Bbk, BXf], [BXf])
                cp("dve", Xb[:], Xf[:], [BXf], [BXb])
                yield
                for lev in range(7):
                    Nk, BNk = M["N%d" % (lev % 2)]
                    Lk, BLk = M["L%d" % (lev % 2)]
                    Nn, BNn = M["N%d" % ((lev + 1) % 2)]
                    Ln, BLn = M["L%d" % ((lev + 1) % 2)]
                    bk, Bbk = nb()
                    for i in range(4):
                        osl = slice(i * 64, (i + 1) * 64)
                        mm(bk[:, osl], Nk[:, isl(i)], Xb[:, osl], True, True, [BNk, BXb], [Bbk])
                    if lev < 6:
                        bk2, Bbk2 = nb()
                        for i in range(4):
                            mm(bk2[:, isl(i)], Lk[:, isl(i)], Nk[:, isl(i)], True, True, [BLk, BNk], [Bbk2])
                        if lev < 5:
                            bk3, Bbk3 = nb()
                            for i in range(4):
                                mm(bk3[:, isl(i)], Nk[:, isl(i)], Lk[:, isl(i)], True, True, [BLk, BNk], [Bbk3])
                    yield
                    tt("dve", Xf[:], bk[:, 0:256], Xf[:], ALU.add, [Bbk, BXf], [BXf])
                    cp("dve", Xb[:], Xf[:], [BXf], [BXb])
                    if lev < 6:
                        cp("act", Nn[:], bk2[:], [Bbk2], [BNn])
                        if lev < 5:
                            cp("act", Ln[:], bk3[:], [Bbk3], [BLn])
                    yield
                yield
                if b >= 1:
                    bk, Bbk = nb()
                    for i, h in enumerate(hs):
                        osl = slice(i * 64, (i + 1) * 64)
                        mm(bk[:, osl], rT[:, hsl(h)], Hb[:, h * 64:(h + 1) * 64], True, True, [BrT, BHb[Q]], [Bbk])
                    for i, h in enumerate(hs):
                        osl = slice(i * 64, (i + 1) * 64)
                        osl2 = slice(256 + i * 64, 256 + (i + 1) * 64)
                        mm(bk[:, osl2], M["RB"][0][:, isl(i)], Xb[:, osl], True, False, [M["RB"][1], BXb], [Bbk])
                        mm(bk[:, osl2], M["RK"][0][:, isl(i)], vb[:, h * 64:(h + 1) * 64], False, True,
                           [M["RK"][1], Bvb], [Bbk])
                    cp("act", y_sb[:, Q * 256:(Q + 1) * 256], bk[:, 0:256], [Bbk], [By])
                    tt("dve", y_sb[:, Q * 256:(Q + 1) * 256], bk[:, 256:512], y_sb[:, Q * 256:(Q + 1) * 256], ALU.add,
                       [Bbk, By], [By])
                yield
                bk, Bbk = nb()
                for i, h in enumerate(hs):
                    osl = slice(i * 64, (i + 1) * 64)
                    mm(bk[0:64, osl], tl["b_h"][:, h * 64:(h + 1) * 64], Xb[:, osl], True, False,
                       [Btl["b_h"], BXb], [Bbk])
                    mm(bk[0:64, osl], tl["k_h"][:, h * 64:(h + 1) * 64], vb[:, h * 64:(h + 1) * 64], False, True,
                       [Btl["k_h"], Bvb], [Bbk])
                hq_sl = slice(Q * 256, (Q + 1) * 256)
                tt("pool", v3(tmpH[:], 4, 64), v3(Hf[:, hq_sl], 4, 64), bc_last(pcfm[:, Q * 4:(Q + 1) * 4], 64),
                   ALU.mult, [BHf[Q], Bpcfm], [BtmpH])
                tt("dve", Hf[:, hq_sl], tmpH[:], bk[0:64, 0:256], ALU.add, [BtmpH, Bbk], [BHf[Q]])
                cp("act", Hb[:, hq_sl], Hf[:, hq_sl], [BHf[Q]], [BHb[Q]])

            extra = [attn_gen(), bonus_gen()] if b >= 1 else []
            for pair in range(2):
                gens = [quad_gen(pair * 2), quad_gen(pair * 2 + 1)] + extra
                nq = 2
                while nq > 0:
                    for g in list(gens):
                        try:
                            next(g)
                        except StopIteration:
                            gens.remove(g)
                            if g in extra:
                                extra.remove(g)
                            else:
                                nq -= 1
            for g in extra:
                for _ in g:
                    pass

        def back_gen(b):
            par = b % 2
            x_t, Bx = xs[par]
            uT, BuT = uTs2[par]
            v_ = p_sb[:, RW0 + 2048:RW0 + 3072]

            def v3(ap, a, bdim):
                return ap.rearrange("p (a b) -> p a b", a=a, b=bdim)
            y3 = v3(y_sb[:], 16, 64)
            red(small[:, 40:56], y3, [By], [Bsm_gn])
            ts("dve", small[:, 40:56], small[:, 40:56], 1.0 / 64, None, ALU.mult, None, [Bsm_gn], [Bsm_gn])
            tt("dve", y3, y3, bc_last(small[:, 40:56], 64), ALU.subtract, [By, Bsm_gn], [By])
            yield "g"
            tt("pool", tA[:], y_sb[:], y_sb[:], ALU.mult, [By], [BtA])
            red(small[:, 40:56], v3(tA[:], 16, 64), [BtA], [Bsm_gn])
            ts("dve", small[:, 40:56], small[:, 40:56], 1.0 / 64, GN_EPS, ALU.mult, ALU.add, [Bsm_gn], [Bsm_gn])
            act(small[:, 40:56], small[:, 40:56], AF.Ln, [Bsm_gn], [Bsm_gn])
            act(small[:, 40:56], small[:, 40:56], AF.Exp, [Bsm_gn], [Bsm_gn], scale=-0.5)
            yield "g"
            tt("dve", y3, y3, bc_last(small[:, 40:56], 64), ALU.mult, [By, Bsm_gn], [By])
            lnw_ap, Blnw = prm("lnw")
            tt("pool", y_sb[:], y_sb[:], lnw_ap, ALU.mult, [By, Blnw], [By])
            yield "g"
            lnb_ap, Blnb = prm("lnb")
            tt("dve", y_sb[:], y_sb[:], lnb_ap, ALU.add, [By, Blnb], [By])
            tt("pool", y_sb[:], y_sb[:], v_, ALU.add, [By, Bp_rw], [By])
            tt("dve", grw[:], y_sb[:], p_sb[:, GR0:GR0 + 1024], ALU.mult, [By, Bp_gr], [Bgrw])
            yield "gn"
            for (src, Bsrc, dst, Bdst) in ((gatt, Bgatt, gattT, BgattT), (grw, Bgrw, grwT, BgrwT)):
                bk, Bbk = nb()
                bkb = bk.bitcast(BF16)
                for c in range(8):
                    tr(bkb[:, c * 128:(c + 1) * 128], src[:, c * 128:(c + 1) * 128], [Bsrc], [Bbk])
                cp("act", dst[:], bkb[:, 0:1024], [Bbk], [Bdst])
                yield
            for br in range(2):
                for j in range(2):
                    wb, Bwb = load_w(win_bf, BWIN, NTM + br * 1024 + j * 512, 512)
                    bk, Bbk = nb()
                    for c in range(8):
                        mm(bk[:], uT[:, c * 128:(c + 1) * 128], wb[:, c, :], c == 0, c == 7, [Bwb, BuT], [Bbk])
                    act(sig[:, j * 512:(j + 1) * 512], bk[:], AF.Sigmoid, [Bbk], [Bsig], waw=(j == 0))
                    yield
                srcT, BsrcT = (gattT, BgattT) if br == 0 else (grwT, BgrwT)
                wsrc_, Bwsrc_ = (wba_bf, BWBA) if br == 0 else (wbr_bf, BWBR)
                dst, Bdst = (t1, Bt1) if br == 0 else (tA, BtA)
                for j in range(2):
                    wb, Bwb = load_w(wsrc_, Bwsrc_, j * 512, 512)
                    bk, Bbk = nb()
                    for c in range(8):
                        mm(bk[:], srcT[:, c * 128:(c + 1) * 128], wb[:, c, :], c == 0, c == 7, [Bwb, BsrcT], [Bbk])
                    tt("dve", dst[:, j * 512:(j + 1) * 512], bk[:], sig[:, j * 512:(j + 1) * 512], ALU.mult,
                       [Bbk, Bsig], [Bdst], waw=(j == 0))
                    yield
            tt("pool", gatt[:], t1[:], tA[:], ALU.add, [Bt1, BtA], [Bgatt])
            bk, Bbk = nb()
            bkb = bk.bitcast(BF16)
            for c in range(8):
                tr(bkb[:, c * 128:(c + 1) * 128], gatt[:, c * 128:(c + 1) * 128], [Bgatt], [Bbk])
            cp("act", mrgT[:], bkb[:, 0:1024], [Bbk], [BmrgT])
            yield
            for j in range(2):
                wb, Bwb = load_w(wout_bf, BWOUT, j * 512, 512)
                bk, Bbk = nb()
                for c in range(8):
                    mm(bk[:], mrgT[:, c * 128:(c + 1) * 128], wb[:, c, :], c == 0, c == 7, [BmrgT, Bwb], [Bbk])
                tt("dve", tB[:, j * 512:(j + 1) * 512], bk[:], x_t[:, j * 512:(j + 1) * 512], ALU.add,
                   [Bbk, Bx], [BtB], waw=False)
                yield
            o_t, Bo = osb[0]
            fnw_ap, Bfnw = prm("fnw")
            rmsnorm_scale(tB[:], BtB, fnw_ap, Bfnw, o_t[:], Bo, so=80, Bs=Bsm_fn)
            P.dma(y[(b - 1) * 128:b * 128, :], o_t[:], Bo, reads=[Bo], writes=[BY])


        def drain(g):
            for _ in g:
                pass

        drain(head_gen(0))
        for b in range(NBLK):
            mid(b)
            bg = back_gen(b) if b >= 1 else None
            hg = head_gen(b + 1) if b + 1 < NBLK else None
            if bg is not None:
                hsteps = 0
                for v_ in bg:
                    if v_ == "gn":
                        break
                    if hg is not None and hsteps < 5:
                        for _ in range(2):
                            if hsteps < 5:
                                next(hg)
                                hsteps += 1
            gens = [g for g in (bg, hg) if g is not None]
            while gens:
                for g in list(gens):
                    try:
                        next(g)
                    except StopIteration:
                        gens.remove(g)

        P.wait_all("sp", [BY])
        P.E["sp"].ops.append(lambda e: e.nop())

        with nc.Block() as block:
            @block.sync
            def _(e):
                for f in P.E["sp"].ops:
                    f(e)

            @block.tensor
            def _(e):
                for f in P.E["pe"].ops:
                    f(e)

            @block.scalar
            def _(e):
                for f in P.E["act"].ops:
                    f(e)

            @block.vector
            def _(e):
                for f in P.E["dve"].ops:
                    f(e)

            @block.gpsimd
            def _(e):
                for f in P.E["pool"].ops:
                    f(e)
        build.stats = {k: len(v.ops) for k, v in P.E.items()}
        build.nsem = P.nsem
        build.sbuf_free = nc.sbuf_bytes_remaining() if callable(getattr(nc, "sbuf_bytes_remaining", None)) else getattr(nc, "sbuf_bytes_remaining", None)
    return nc


def make_consts():
    i = np.arange(128)
    ident = np.eye(128, dtype=np.float32)
    mc = (i[:, None] <= i[None, :]).astype(np.float32)
    mp = (i[:, None] > i[None, :]).astype(np.float32)
    mp1 = mp * (i[:, None] >= PAD).astype(np.float32)
    su = (i[:, None] < i[None, :]).astype(np.float32)
    sl = (i[:, None] > i[None, :]).astype(np.float32)
    maskA = np.concatenate([mc, mp, mc, mp], 1)
    maskA1 = np.concatenate([mc, mp1, mc, mp1], 1)
    SU4 = np.tile(su, (1, 4))
    IU4 = np.tile(mc, (1, 4))
    SL4 = np.tile(sl, (1, 4))
    tri = mc
    ones = np.ones((128, 128), np.float32)
    sh = (i[:, None] + 1 == i[None, :]).astype(np.float32)
    em = np.zeros((128, 128), np.float32)
    em[127, 0] = 1.0
    c = np.concatenate([ident, maskA, maskA1, SU4, IU4, SL4, tri, ones, sh, em], 1).astype(np.float32)
    assert c.shape == (128, NCONST)
    return np.ascontiguousarray(c)


def make_tabs(NBLK):
    half = 32
    inv = (1.0 / (np.float32(10000.0) ** (np.arange(half, dtype=np.float32) / np.float32(half)))).astype(np.float32)
    pos = (np.arange(NBLK * 128, dtype=np.float32) - np.float32(PAD)).astype(np.float32)
    ang = (pos[:, None] * inv[None, :]).astype(np.float32)
    return np.ascontiguousarray(np.concatenate([np.cos(ang), np.sin(ang)], 1).astype(np.float32))


_CACHE = {}


def run(inputs, NBLK, ncores):
    f = lambda a: np.ascontiguousarray(np.asarray(a, dtype=np.float32))
    x = f(inputs["x"])
    meta = f(inputs["meta_tokens"])
    nx = (NBLK - 1) * 128
    if NBLK not in _CACHE:
        _CACHE[NBLK] = build(NBLK)
    nc = _CACHE[NBLK]
    common = {
        "w_in": f(inputs["w_in"][0]), "w_ba": f(inputs["w_branch_att"][0]), "w_br": f(inputs["w_branch_rwkv"][0]),
        "w_out": f(inputs["w_out"][0]), "consts": make_consts(), "tabs": make_tabs(NBLK),
        "p_normw": f(inputs["norm_w"][0]).reshape(1, -1), "p_fnw": f(inputs["final_norm_w"]).reshape(1, -1),
        "p_mu": f(inputs["rk_mu"][0]).reshape(1, -1), "p_w0": f(inputs["rk_w0"][0]).reshape(1, -1),
        "p_a0": f(inputs["rk_a0"][0]).reshape(1, -1), "p_kk": f(inputs["rk_k_k"][0]).reshape(1, -1),
        "p_ka": f(inputs["rk_k_a"][0]).reshape(1, -1), "p_rk": f(inputs["rk_r_k"][0]).reshape(1, -1),
        "p_lnw": f(inputs["rk_ln_w"][0]).reshape(1, -1), "p_lnb": f(inputs["rk_ln_b"][0]).reshape(1, -1),
        "p_sinks": f(inputs["att_sinks"][0]).reshape(1, -1),
        "w2": f(inputs["rk_w2"][0]), "a2": f(inputs["rk_a2"][0]),
    }
    in_maps = []
    for cidx in range(ncores):
        h0 = np.zeros((NBLK * 128, D), np.float32)
        h0[PAD:128] = meta
        h0[128:] = x[cidx, :nx]
        m = dict(common)
        m["h0"] = h0
        in_maps.append(m)
    res = run_bass_kernel_spmd(nc, in_maps, core_ids=list(range(ncores)))
    return np.stack([np.asarray(r["y"], dtype=np.float32) for r in res.results], 0)


def kernel(**inputs):
    return run(inputs, 65, 8)
```
